# Optimizing a Trainium2 kernel written in Bass

```python
import math
import jax, jax.numpy as jnp
from jax import lax
import numpy as np

D_MODEL = 1024
BATCH = 16
SEQ = 256
DEPTH = 2
DEC_BATCH = 2
DEC_SEQ = 4096
PAST_LEN = 256

GRID_W = 64
HEAD_DIM = 64
A_HEADS = 6
A_KV = 2
B_HEADS = 6
B_KV = 2
C_HEADS = 4
C_QK_DIM = 32
C_V_DIM = 2 * C_QK_DIM
WINDOW = 128
Q_BLOCK = 128
ROPE_THETA = 10000.0
D_FF = 2816
CONV_W = 3
EPS = 1e-6
NEG_INF = -1e30
IN_SIZES = (A_HEADS * HEAD_DIM, A_KV * HEAD_DIM, A_KV * HEAD_DIM,
            B_HEADS * HEAD_DIM, B_KV * HEAD_DIM, B_KV * HEAD_DIM,
            C_HEADS * 2 * C_QK_DIM, C_HEADS * 2 * C_QK_DIM, C_HEADS * C_V_DIM)
D_IN = 2048
MIX_WIDTH = A_HEADS * HEAD_DIM + B_HEADS * HEAD_DIM + C_HEADS * C_V_DIM

kernel_name = 'hybrid_diffusion_prefix_step'


def rms_norm(x, g):
    xf = x.astype(jnp.float32)
    y = xf * lax.rsqrt(jnp.mean(xf * xf, axis=-1, keepdims=True) + EPS)
    return (y * g.astype(jnp.float32)).astype(x.dtype)


def modulation(cond, w, b):
    m = jax.nn.silu(cond) @ w + b
    return jnp.split(m, 6, axis=-1)


def rope_1d(x, pos):
    half = x.shape[-1] // 2
    inv = ROPE_THETA ** (-jnp.arange(half, dtype=jnp.float32) / half)
    ang = pos.astype(jnp.float32)[:, None] * inv[None, :]
    shape = (1, pos.shape[0]) + (1,) * (x.ndim - 3) + (half,)
    cos = jnp.cos(ang).reshape(shape)
    sin = jnp.sin(ang).reshape(shape)
    xf = x.astype(jnp.float32)
    x1, x2 = xf[..., :half], xf[..., half:]
    return jnp.concatenate([x1 * cos - x2 * sin, x2 * cos + x1 * sin], axis=-1).astype(x.dtype)


def rope_2d(x, rows, cols):
    h = x.shape[-1] // 2
    return jnp.concatenate([rope_1d(x[..., :h], rows), rope_1d(x[..., h:], cols)], axis=-1)


def to_blocks(a):
    B, T = a.shape[0], a.shape[1]
    return a.reshape((B, T // Q_BLOCK, Q_BLOCK) + a.shape[2:]).swapaxes(0, 1)


def from_blocks(a):
    nb, B = a.shape[0], a.shape[1]
    return a.swapaxes(0, 1).reshape((B, nb * Q_BLOCK) + a.shape[3:])


def gqa_dense_blocks(q, k, v, sink=None):
    B, T, H, d = q.shape
    KV = k.shape[2]
    G = H // KV
    scale = d ** -0.5

    def block(qb):
        qg = qb.reshape(B, Q_BLOCK, KV, G, d)
        s = jnp.einsum('bqkgd,bskd->bkgqs', qg, k).astype(jnp.float32) * scale
        if sink is not None:
            sk = jnp.broadcast_to(sink.astype(jnp.float32).reshape(1, KV, G, 1, 1), s.shape[:-1] + (1,))
            p = jax.nn.softmax(jnp.concatenate([s, sk], axis=-1), axis=-1)[..., :-1]
        else:
            p = jax.nn.softmax(s, axis=-1)
        o = jnp.einsum('bkgqs,bskd->bqkgd', p.astype(v.dtype), v)
        return o.reshape(B, Q_BLOCK, H, d)

    return from_blocks(lax.map(block, to_blocks(q)))


def diff_dense_blocks(q1, q2, k1, k2, v, lam):
    scale = q1.shape[-1] ** -0.5

    def block(qs):
        q1b, q2b = qs
        p1 = jax.nn.softmax(jnp.einsum('bqhd,bshd->bhqs', q1b, k1).astype(jnp.float32) * scale, axis=-1)
        p2 = jax.nn.softmax(jnp.einsum('bqhd,bshd->bhqs', q2b, k2).astype(jnp.float32) * scale, axis=-1)
        p = p1 - lam * p2
        return jnp.einsum('bhqs,bshe->bqhe', p.astype(v.dtype), v)

    return from_blocks(lax.map(block, (to_blocks(q1), to_blocks(q2))))


def window_attn_latent(q, k, v, k_ctx, v_ctx, sink):
    B, T, H, d = q.shape
    KV = k.shape[2]
    G = H // KV
    nb = T // Q_BLOCK
    scale = d ** -0.5
    pad = ((0, 0), (WINDOW, WINDOW), (0, 0), (0, 0))
    kp = jnp.pad(k, pad).reshape(B, nb + 2, Q_BLOCK, KV, d)
    vp = jnp.pad(v, pad).reshape(B, nb + 2, Q_BLOCK, KV, d)
    k_band = jnp.concatenate([kp[:, :-2], kp[:, 1:-1], kp[:, 2:]], axis=2)
    v_band = jnp.concatenate([vp[:, :-2], vp[:, 1:-1], vp[:, 2:]], axis=2)
    qg = q.reshape(B, nb, Q_BLOCK, KV, G, d)
    s_band = jnp.einsum('bnqkgd,bnskd->bnkgqs', qg, k_band).astype(jnp.float32) * scale
    qi = jnp.arange(Q_BLOCK)[:, None]
    sj = jnp.arange(3 * Q_BLOCK)[None, :]
    rel = sj - qi - Q_BLOCK
    kpos = jnp.arange(nb)[:, None, None] * Q_BLOCK - Q_BLOCK + sj[None]
    allowed = (jnp.abs(rel) <= WINDOW)[None] & (kpos >= 0) & (kpos < T)
    s_band = jnp.where(allowed[None, :, None, None], s_band, NEG_INF)
    s_ctx = jnp.einsum('bnqkgd,bskd->bnkgqs', qg, k_ctx).astype(jnp.float32) * scale
    sk = jnp.broadcast_to(sink.astype(jnp.float32).reshape(1, 1, KV, G, 1, 1), s_band.shape[:-1] + (1,))
    p = jax.nn.softmax(jnp.concatenate([s_band, s_ctx, sk], axis=-1), axis=-1)
    n_band = 3 * Q_BLOCK
    n_ctx = k_ctx.shape[1]
    o = (jnp.einsum('bnkgqs,bnskd->bnqkgd', p[..., :n_band].astype(v.dtype), v_band)
         + jnp.einsum('bnkgqs,bskd->bnqkgd', p[..., n_band:n_band + n_ctx].astype(v.dtype), v_ctx))
    return o.reshape(B, T, H, d)


def project(h, w_in):
    z = h @ w_in
    B, T = z.shape[0], z.shape[1]
    parts = jnp.split(z, np.cumsum(IN_SIZES)[:-1].tolist(), axis=-1)
    shapes = ((A_HEADS, HEAD_DIM), (A_KV, HEAD_DIM), (A_KV, HEAD_DIM),
              (B_HEADS, HEAD_DIM), (B_KV, HEAD_DIM), (B_KV, HEAD_DIM),
              (C_HEADS, 2, C_QK_DIM), (C_HEADS, 2, C_QK_DIM), (C_HEADS, C_V_DIM))
    return tuple(p.reshape((B, T) + s) for p, s in zip(parts, shapes))


def diff_lambda(lq1, lk1, lq2, lk2, lam_init):
    f = lambda a, b: jnp.exp(jnp.sum(a.astype(jnp.float32) * b.astype(jnp.float32)))
    return f(lq1, lk1) - f(lq2, lk2) + lam_init


def merge_out(oa, ob, oc, g_sub, lam_init, w_out):
    B, T = oa.shape[0], oa.shape[1]
    oc = rms_norm(oc, g_sub) * (1.0 - lam_init)
    o = jnp.concatenate([oa.reshape(B, T, -1), ob.reshape(B, T, -1), oc.reshape(B, T, -1)], axis=-1)
    return o @ w_out


def conv_ffn(h, w_up, conv_w, conv_b, w_down):
    u = h @ w_up
    up = jnp.pad(u, ((0, 0), (1, 1), (0, 0)))
    u = conv_w[0] * up[:, :-2] + conv_w[1] * up[:, 1:-1] + conv_w[2] * up[:, 2:] + conv_b
    a, g = jnp.split(u, 2, axis=-1)
    return (jax.nn.silu(a) * g) @ w_down


def setup_inputs(seed: int = 0) -> dict:
    key = jax.random.key(seed)
    ks = jax.random.split(key, 30)

    def nrm(k, shape, s=1.0):
        return jax.random.normal(k, shape, jnp.float32) * s

    def gain(k, shape):
        return 1.0 + nrm(k, shape, 0.05)

    return {
        'x_prompt': nrm(ks[0], (BATCH, SEQ, D_MODEL)),
        'x_sample': nrm(ks[1], (DEC_BATCH, DEC_SEQ, D_MODEL)),
        'cache_a_k': nrm(ks[2], (DEC_BATCH, DEPTH, PAST_LEN, A_KV, HEAD_DIM)),
        'cache_a_v': nrm(ks[3], (DEC_BATCH, DEPTH, PAST_LEN, A_KV, HEAD_DIM)),
        'cache_b_k': nrm(ks[4], (DEC_BATCH, DEPTH, PAST_LEN, B_KV, HEAD_DIM)),
        'cache_b_v': nrm(ks[5], (DEC_BATCH, DEPTH, PAST_LEN, B_KV, HEAD_DIM)),
        'cache_c_k': nrm(ks[6], (DEC_BATCH, DEPTH, PAST_LEN, C_HEADS, 2 * C_QK_DIM)),
        'cache_c_v': nrm(ks[7], (DEC_BATCH, DEPTH, PAST_LEN, C_HEADS, C_V_DIM)),
        'c': nrm(ks[8], (DEC_BATCH, D_MODEL)),
        'c_ctx': nrm(ks[9], (D_MODEL,)),
        'w_ada': nrm(ks[10], (DEPTH, D_MODEL, 6 * D_MODEL), D_MODEL ** -0.5),
        'b_ada': nrm(ks[11], (DEPTH, 6 * D_MODEL), 0.02),
        'g_norm1': gain(ks[12], (DEPTH, D_MODEL)),
        'g_norm2': gain(ks[13], (DEPTH, D_MODEL)),
        'w_in': nrm(ks[14], (DEPTH, D_MODEL, D_IN), D_MODEL ** -0.5),
        'g_qa': gain(ks[15], (DEPTH, HEAD_DIM)),
        'g_ka': gain(ks[16], (DEPTH, HEAD_DIM)),
        'sink_b': nrm(ks[17], (DEPTH, B_HEADS), 0.5),
        'lam_q1': nrm(ks[18], (DEPTH, C_QK_DIM), 0.1),
        'lam_k1': nrm(ks[19], (DEPTH, C_QK_DIM), 0.1),
        'lam_q2': nrm(ks[20], (DEPTH, C_QK_DIM), 0.1),
        'lam_k2': nrm(ks[21], (DEPTH, C_QK_DIM), 0.1),
        'g_subln': gain(ks[22], (DEPTH, C_V_DIM)),
        'w_out': nrm(ks[23], (DEPTH, MIX_WIDTH, D_MODEL), MIX_WIDTH ** -0.5),
        'w_up': nrm(ks[24], (DEPTH, D_MODEL, 2 * D_FF), D_MODEL ** -0.5),
        'conv_w': nrm(ks[25], (DEPTH, CONV_W, 2 * D_FF), CONV_W ** -0.5),
        'conv_b': nrm(ks[26], (DEPTH, 2 * D_FF), 0.02),
        'w_down': nrm(ks[27], (DEPTH, D_FF, D_MODEL), D_FF ** -0.5),
        'g_final': gain(ks[28], (D_MODEL,)),
    }


def reference(x_prompt, x_sample, cache_a_k, cache_a_v, cache_b_k, cache_b_v, cache_c_k, cache_c_v,
              c, c_ctx, w_ada, b_ada, g_norm1, g_norm2, w_in, g_qa, g_ka, sink_b,
              lam_q1, lam_k1, lam_q2, lam_k2, g_subln, w_out, w_up, conv_w, conv_b, w_down, g_final):
    x = x_prompt
    Bp, Tp = x.shape[0], x.shape[1]
    cond = c_ctx[None, None, :]
    ak_l, av_l, bk_l, bv_l, ck_l, cv_l = [], [], [], [], [], []
    for l in range(DEPTH):
        lam_init = 0.8 - 0.6 * math.exp(-0.3 * l)
        sh1, sc1, gt1, sh2, sc2, gt2 = modulation(cond, w_ada[l], b_ada[l])
        h = rms_norm(x, g_norm1[l]) * (1 + sc1) + sh1
        qa, ka, va, qb, kb, vb, qc, kc, vc = project(h, w_in[l])
        qa = rms_norm(qa, g_qa[l])
        ka = rms_norm(ka, g_ka[l])
        lam = diff_lambda(lam_q1[l], lam_k1[l], lam_q2[l], lam_k2[l], lam_init)
        oa = gqa_dense_blocks(qa, ka, va)
        ob = gqa_dense_blocks(qb, kb, vb, sink_b[l])
        oc = diff_dense_blocks(qc[..., 0, :], qc[..., 1, :], kc[..., 0, :], kc[..., 1, :], vc, lam)
        x = x + gt1 * merge_out(oa, ob, oc, g_subln[l], lam_init, w_out[l])
        h = rms_norm(x, g_norm2[l]) * (1 + sc2) + sh2
        x = x + gt2 * conv_ffn(h, w_up[l], conv_w[l], conv_b[l], w_down[l])
        ak_l.append(ka)
        av_l.append(va)
        bk_l.append(kb)
        bv_l.append(vb)
        ck_l.append(kc.reshape(Bp, Tp, C_HEADS, 2 * C_QK_DIM))
        cv_l.append(vc)
    y_prompt = rms_norm(x, g_final)
    new_a_k = jnp.stack(ak_l, axis=1)
    new_a_v = jnp.stack(av_l, axis=1)
    new_b_k = jnp.stack(bk_l, axis=1)
    new_b_v = jnp.stack(bv_l, axis=1)
    new_c_k = jnp.stack(ck_l, axis=1)
    new_c_v = jnp.stack(cv_l, axis=1)

    x = x_sample
    Bs, Ts = x.shape[0], x.shape[1]
    P = cache_c_k.shape[2]
    t = jnp.arange(Ts)
    rows = t // GRID_W
    cols = t % GRID_W
    cond = c[:, None, :]
    for l in range(DEPTH):
        lam_init = 0.8 - 0.6 * math.exp(-0.3 * l)
        sh1, sc1, gt1, sh2, sc2, gt2 = modulation(cond, w_ada[l], b_ada[l])
        h = rms_norm(x, g_norm1[l]) * (1 + sc1) + sh1
        qa, ka, va, qb, kb, vb, qc, kc, vc = project(h, w_in[l])
        qa = rope_2d(rms_norm(qa, g_qa[l]), rows, cols)
        ka = rope_2d(rms_norm(ka, g_ka[l]), rows, cols)
        qb = rope_2d(qb, rows, cols)
        kb = rope_2d(kb, rows, cols)
        qc = rope_2d(qc, rows, cols)
        kc = rope_2d(kc, rows, cols)
        lam = diff_lambda(lam_q1[l], lam_k1[l], lam_q2[l], lam_k2[l], lam_init)
        ak = jnp.concatenate([cache_a_k[:, l], ka], axis=1)
        av = jnp.concatenate([cache_a_v[:, l], va], axis=1)
        oa = gqa_dense_blocks(qa, ak, av)
        ob = window_attn_latent(qb, kb, vb, cache_b_k[:, l], cache_b_v[:, l], sink_b[l])
        kc_all = jnp.concatenate([cache_c_k[:, l].reshape(Bs, P, C_HEADS, 2, C_QK_DIM), kc], axis=1)
        vc_all = jnp.concatenate([cache_c_v[:, l], vc], axis=1)
        oc = diff_dense_blocks(qc[..., 0, :], qc[..., 1, :], kc_all[..., 0, :], kc_all[..., 1, :], vc_all, lam)
        x = x + gt1 * merge_out(oa, ob, oc, g_subln[l], lam_init, w_out[l])
        h = rms_norm(x, g_norm2[l]) * (1 + sc2) + sh2
        x = x + gt2 * conv_ffn(h, w_up[l], conv_w[l], conv_b[l], w_down[l])
    y_sample = rms_norm(x, g_final)

    return (y_prompt, y_sample, new_a_k, new_a_v, new_b_k, new_b_v, new_c_k, new_c_v)
```

```python
from contextlib import ExitStack
import math
import numpy as np
import ml_dtypes
import concourse.bass as bass
import concourse.mybir as mybir
from concourse.bass_utils import run_bass_kernel_spmd

F32 = mybir.dt.float32
BF16 = mybir.dt.bfloat16
AF = mybir.ActivationFunctionType
ALU = mybir.AluOpType
AX = mybir.AxisListType


class Sched:
    ENGS = ("pe", "act", "dve", "pool", "sp")
    NDS = {"sp": 24, "pool": 12}
    CAP = 30000

    def __init__(self, nc, same_eng_sync=True):
        self.nc = nc
        self.prog = {e: [] for e in self.ENGS}
        self.hist = {}
        self.ndma = {"sp": 0, "pool": 0}
        self.same_eng_sync = same_eng_sync

    @staticmethod
    def region(ap):
        t = ap.tensor
        shape = [int(s) for s in t.shape]
        off = int(ap.offset)
        dims = [(int(a), int(b)) for a, b in ap.ap]
        if isinstance(t, bass.DRamTensorHandle):
            ext = sum((c - 1) * abs(s) for s, c in dims) + 1
            return (t.name, 0, 1, off, off + ext)
        row = 1
        for s in shape[1:]:
            row *= s
        p0 = off // row
        f0 = off % row
        p1 = p0 + dims[0][1]
        ext = sum((c - 1) * abs(s) for s, c in dims[1:]) + 1
        return (t.name, p0, p1, f0, f0 + ext)

    def _track(self, who, reads, writes):
        deps = set()
        regs_r = [self.region(a) for a in reads]
        regs_w = [self.region(a) for a in writes]
        for (name, p0, p1, f0, f1) in regs_r:
            for rec in self.hist.get(name, ()):
                if rec[6] and rec[2] < p1 and p0 < rec[3] and rec[4] < f1 and f0 < rec[5]:
                    deps.add(rec[0])
        for (name, p0, p1, f0, f1) in regs_w:
            for rec in self.hist.get(name, ()):
                if rec[2] < p1 and p0 < rec[3] and rec[4] < f1 and f0 < rec[5]:
                    deps.add(rec[0])
        for (name, p0, p1, f0, f1) in regs_w:
            lst = self.hist.setdefault(name, [])
            lst[:] = [r for r in lst if not (p0 <= r[2] and r[3] <= p1 and f0 <= r[4] and r[5] <= f1)]
            lst.append((who, None, p0, p1, f0, f1, True))
        for (name, p0, p1, f0, f1) in regs_r:
            lst = self.hist.setdefault(name, [])
            lst[:] = [r for r in lst if not ((not r[6]) and r[0][0] == who[0] and r[0][0] != "dma"
                                             and r[2] == p0 and r[3] == p1 and r[4] == f0 and r[5] == f1)]
            lst.append((who, None, p0, p1, f0, f1, False))
        deps.discard(who)
        return deps

    def op(self, eng, fn, reads=(), writes=()):
        idx = len(self.prog[eng])
        who = (eng, idx)
        deps = self._track(who, reads, writes)
        if eng == "pe":
            deps = {d for d in deps if d[0] != "pe"}
        elif not self.same_eng_sync:
            deps = {d for d in deps if d[0] != eng}
        self.prog[eng].append({"kind": "op", "fn": fn, "deps": deps})
        return who

    def dma(self, q, out, in_, **kw):
        k = self.ndma[q]
        self.ndma[q] += 1
        who = ("dma", q, k)
        deps = self._track(who, [in_], [out])
        self.prog[q].append({"kind": "dma", "fn": (lambda e: e.dma_start(out=out, in_=in_, **kw)),
                             "deps": deps, "k": k})
        return who

    def collective(self, fn, reads, writes):
        k = self.ndma["pool"]
        self.ndma["pool"] += 1
        who = ("dma", "pool", k)
        deps = self._track(who, reads, writes)
        self.prog["pool"].append({"kind": "cc", "fn": fn, "deps": deps, "k": k})
        return who

    def emit(self, es):
        nc = self.nc
        needed = {e: set() for e in self.ENGS}
        for e in self.ENGS:
            for ins in self.prog[e]:
                for d in ins["deps"]:
                    if d[0] != "dma":
                        needed[d[0]].add(d[1])
        cnt = {}
        nsig = {}
        for e in self.ENGS:
            c = 0
            for i in range(len(self.prog[e])):
                if i in needed[e]:
                    c += 1
                    cnt[(e, i)] = c
            nsig[e] = c
        csem = {}
        for e in ("pe", "act", "dve", "pool"):
            n = max(1, (nsig[e] + self.CAP - 1) // self.CAP)
            csem[e] = [es.enter_context(nc.semaphore(f"c_{e}_{j}")) for j in range(n)]
        dsem = {q: [es.enter_context(nc.semaphore(f"d_{q}_{j}")) for j in range(self.NDS[q])]
                for q in ("sp", "pool")}
        ccs = {}
        for ins in self.prog["pool"]:
            if ins["kind"] == "cc":
                ccs[ins["k"]] = es.enter_context(nc.semaphore(f"cc_{ins['k']}"))
        ncc = len(ccs)
        dma_target = {}
        for q in ("sp", "pool"):
            per = [0] * self.NDS[q]
            nd = 0
            for ins in self.prog[q]:
                if ins["kind"] == "dma":
                    s = nd % self.NDS[q]
                    nd += 1
                    per[s] += 1
                    dma_target[(q, ins["k"])] = (dsem[q][s], 16 * per[s], 16 * (per[s] - 1))
                elif ins["kind"] == "cc":
                    dma_target[(q, ins["k"])] = (ccs[ins["k"]], 1, 0)

        def target(d):
            if d[0] == "dma":
                t = dma_target[(d[1], d[2])]
                return t[0], t[1]
            c = cnt[d] - 1
            return csem[d[0]][c // self.CAP], c % self.CAP + 1

        import os
        if os.environ.get("MK_STATS"):
            print("STATS nsig", nsig, "ndma", self.ndma, "ninstr", {e: len(self.prog[e]) for e in self.ENGS}, "ncc", ncc, flush=True)
        block = es.enter_context(nc.Block())

        def make(eng):
            def body(e):
                waited = {}
                for i, ins in enumerate(self.prog[eng]):
                    tg = {}
                    for d in ins["deps"]:
                        s, v = target(d)
                        key = id(s)
                        if key not in tg or tg[key][1] < v:
                            tg[key] = (s, v)
                    if ins["kind"] in ("dma", "cc"):
                        s, v, prev = dma_target[(eng, ins["k"])]
                        if prev > 0:
                            key = id(s)
                            if key not in tg or tg[key][1] < prev:
                                tg[key] = (s, prev)
                    for key, (s, v) in tg.items():
                        if waited.get(key, 0) < v:
                            e.wait_ge(s, v)
                            waited[key] = v
                    r = ins["fn"](e)
                    if ins["kind"] == "dma":
                        r.then_inc(dma_target[(eng, ins["k"])][0], 16)
                    elif ins["kind"] == "cc":
                        r.then_inc(dma_target[(eng, ins["k"])][0], 1)
                    elif (eng, i) in cnt:
                        c = cnt[(eng, i)] - 1
                        r.then_inc(csem[eng][c // self.CAP], 1)
                if eng in ("sp", "pool"):
                    last = {}
                    for ins in self.prog[eng]:
                        if ins["kind"] in ("dma", "cc"):
                            s, v, _ = dma_target[(eng, ins["k"])]
                            last[id(s)] = (s, v)
                    for key, (s, v) in last.items():
                        if waited.get(key, 0) < v:
                            e.wait_ge(s, v)
            return body

        block.tensor(make("pe"))
        block.scalar(make("act"))
        block.vector(make("dve"))
        block.gpsimd(make("pool"))
        block.sync(make("sp"))


class DummySched:
    def op(self, *a, **k):
        return None

    def dma(self, *a, **k):
        return None

    def collective(self, *a, **k):
        return None


class WStream:
    NSLOT = 7
    LOOK = 6

    def __init__(self, slots):
        self.slots = slots
        self.order = []
        self.mode = "record"
        self.i = 0
        self.issued = 0

    def play(self):
        self.mode = "play"
        self.i = 0
        self.issued = 0

    def use(self, S, src, kdim, n, look=None):
        if self.mode == "record":
            self.order.append((src, kdim, n))
            i = len(self.order) - 1
        else:
            i = self.i
            self.i += 1
            la = self.LOOK if look is None else look
            while self.issued < min(len(self.order), i + 1 + la):
                s2, k2, n2 = self.order[self.issued]
                S.dma("pool", self._view(self.issued, k2, n2), s2)
                self.issued += 1
        return self._view(i, kdim, n)

    def _view(self, i, kdim, n):
        sl = self.slots[i % self.NSLOT]
        return sl[:, 0:kdim * n].rearrange("p (k n) -> p k n", n=n)


D = 1024
DEPTH = 2
DFF = 2816
NJ = 22
EPS = 1e-6
THETA = 10000.0
LAM_INIT = [0.8 - 0.6 * math.exp(-0.3 * l) for l in range(DEPTH)]
CK_AK, CK_BK, CK_CK0, CK_CK1, CK_A0, CK_B0, CK_C0 = 0, 1, 2, 3, 4, 7, 10
NQK = 12
NPV = 8 * 4 + 8 + 96 + 4 + 2 + 2 * NJ * 2 * 4 + 6
PV_GN, PV_GF, PV_BADA, PV_GQK, PV_GSUB, PV_CW, PV_SINK = 0, 32, 40, 136, 140, 142, 142 + 2 * NJ * 2 * 4
PV_SINKX = PV_SINK + 6
NPVEC = PV_SINKX + 6


def build_program(run_s=True):
    nc = bass.Bass("TRN2", target_bir_lowering=False)

    def din(name, shape, dt=F32):
        return nc.dram_tensor(name, list(shape), dt, kind="ExternalInput").ap()

    def dout(name, shape, dt=F32):
        return nc.dram_tensor(name, list(shape), dt, kind="ExternalOutput").ap()

    xp = din("xp", [512, D])
    xs = din("xs", [1024, D])
    cak = din("cakT", [DEPTH, 128, 256]); cav = din("cav", [DEPTH, 256, 128])
    cbk = din("cbkT", [DEPTH, 128, 256]); cbv = din("cbv", [DEPTH, 256, 128])
    cck = din("cckT", [DEPTH, 256, 256]); ccv = din("ccv", [DEPTH, 256, 256])
    cond2T = din("cond2T", [128, 8, 2])
    w_ada = din("w_ada", [DEPTH, D, 6 * D])
    w_inp = din("w_inp", [DEPTH, D, 2560])
    w_outp = din("w_outp", [DEPTH, D, D])
    w_upp = din("w_upp", [DEPTH, D, NJ, 256])
    w_downp = din("w_downp", [DEPTH, DFF, D])
    pvec_d = din("pvec", [128, NPVEC])
    lamv_d = din("lamv", [1, DEPTH * 4 * 32])
    gka_d = din("gka_row", [DEPTH, 64])
    cmat_d = din("cmat", [128, 4, 128], BF16)
    rope_d = din("rope", [128, 4, 1024], BF16)
    bmask_d = din("bmask", [128, 4, 128], BF16)
    sel_d = din("sel", [128, 8])

    y_p = dout("y_p", [512, D])
    y_s = dout("y_s", [1024, D])
    o_ak = dout("o_ak", [2, DEPTH, 256, 128]); o_av = dout("o_av", [2, DEPTH, 256, 128])
    o_bk = dout("o_bk", [2, DEPTH, 256, 128]); o_bv = dout("o_bv", [2, DEPTH, 256, 128])
    o_ck = dout("o_ck", [2, DEPTH, 256, 256]); o_cv = dout("o_cv", [2, DEPTH, 256, 256])

    aginK = [nc.dram_tensor(f"aginK{l}", [512, 1024], BF16) for l in range(DEPTH)]
    agoutK = [nc.dram_tensor(f"agoutK{l}", [2048, 1024], BF16) for l in range(DEPTH)]
    aginV = [[nc.dram_tensor(f"aginV{l}_{i}", [384, 1024], BF16) for i in range(2)] for l in range(DEPTH)]
    agoutV = [[nc.dram_tensor(f"agoutV{l}_{i}", [1536, 1024], BF16) for i in range(2)] for l in range(DEPTH)]
    hin = [nc.dram_tensor(f"hin{l}", [128, 16], BF16) for l in range(DEPTH)]
    hout = [nc.dram_tensor(f"hout{l}", [512, 16], BF16) for l in range(DEPTH)]
    RG = [[0, 1, 2, 3], [4, 5, 6, 7]]

    with ExitStack() as es:
        def sb(name, shape, dt):
            return es.enter_context(nc.sbuf_tensor(name, list(shape), dt))

        xTp = sb("xTp", [128, 8, 512], F32)
        xTs = sb("xTs", [128, 8, 1026], F32)
        hT = sb("hT", [128, 8, 1026], BF16)
        qk = sb("qk", [128, NQK, 1026], BF16)
        rope = sb("ropet", [128, 4, 1024], BF16)
        gK = sb("gK", [128, 2, 4352], BF16)
        gV = sb("gV", [128, 34, 192], BF16)
        vloc = sb("vloc", [128, 3072], BF16)
        kBx = sb("kBx", [128, 4, 128], BF16)
        vB = sb("vB", [128, 13, 192], BF16)
        candH = sb("candH", [128, 4, 16], BF16)
        hst = sb("hst", [128, 16], BF16)
        PT = [sb(f"PT{i}", [128, 1024], BF16) for i in range(3)]
        wsl = [sb(f"wsl{i}", [128, 2048], BF16) for i in range(7)]
        tfp = sb("tfp", [128, 3, 1026], F32)
        tb = sb("tb", [128, 3, 1026], BF16)
        rst = sb("rst", [128, 2, 512], F32)
        ocT = sb("ocT", [128, 512], F32)
        cmat = sb("cmat_s", [128, 4, 128], BF16)
        bmask = sb("bmask_s", [128, 4, 128], BF16)
        identF = sb("identF", [128, 128], F32)
        identB = sb("identB", [128, 128], BF16)
        pvec = sb("pvec_s", [128, NPVEC], F32)
        modT = sb("modT", [128, DEPTH, 48, 2], F32)
        amod = sb("amod", [128, DEPTH, 2, 2, 8], F32)
        scT = sb("scT", [128, 8, 2], BF16)
        cstage = sb("cstage", [128, 8, 2], F32)
        lamw = sb("lamw", [128, DEPTH * 4 * 32], F32)
        lams = sb("lams", [128, 16], F32)
        gsub2 = sb("gsub2", [128, DEPTH], F32)
        esink = sb("esink", [128, 12], F32)
        gkar = sb("gkar", [128, DEPTH, 64], F32)
        sel = sb("sel_s", [128, 8], F32)
        epsc = sb("epsc", [128, 1], F32)
        sst = sb("sst", [128, 8], F32)
        qm = sb("qm", [128, 2, 512], BF16)
        ps = es.enter_context(nc.psum_tensor("ps", [128, 8, 512], F32))
        W = WStream(wsl)

        R64, R32, BD64, ONES = (cmat[:, i, :] for i in range(4))

        def program(S):
            def bank(b, w=512):
                return ps[:, b, 0:w]

            S.dma("sp", cmat[:, :, :], cmat_d)
            S.dma("sp", bmask[:, :, :], bmask_d)
            S.dma("sp", rope[:, :, :], rope_d)
            S.dma("sp", pvec[:, :], pvec_d)
            S.dma("sp", sel[:, :], sel_d)
            S.dma("sp", cstage[:, :, :], cond2T)
            S.dma("sp", lamw[:, :], lamv_d.partition_broadcast(128))
            for l in range(DEPTH):
                S.dma("sp", gkar[:, l, :], gka_d[l:l + 1, :].partition_broadcast(128))
            S.op("pool", lambda e: e.memset(identF[:, :], 1.0), [], [identF[:, :]])
            S.op("pool", lambda e: e.affine_select(out=identF[:, :], in_=identF[:, :], pattern=[[-1, 128]],
                                                    compare_op=ALU.is_equal, fill=0.0, base=0, channel_multiplier=1),
                 [identF[:, :]], [identF[:, :]])
            S.op("pool", lambda e: e.memset(epsc[:, :], EPS), [], [epsc[:, :]])
            S.op("pool", lambda e: e.tensor_copy(out=identB[:, :], in_=identF[:, :]), [identF[:, :]], [identB[:, :]])
            S.op("pool", lambda e: e.memset(gV[:, :, 64:128], 1.0), [], [gV[:, :, 64:128]])
            S.op("pool", lambda e: e.memset(vB[:, :, 64:128], 1.0), [], [vB[:, :, 64:128]])
            S.op("pool", lambda e: e.memset(qm[:, :, :], 0.0), [], [qm[:, :, :]])
            S.op("act", lambda e: e.activation(out=scT[:, :, :], in_=cstage[:, :, :], func=AF.Silu),
                 [cstage[:, :, :]], [scT[:, :, :]])
            S.op("act", lambda e: e.activation(out=esink[:, :], in_=pvec[:, PV_SINK:PV_SINK + 12], func=AF.Exp),
                 [pvec[:, PV_SINK:PV_SINK + 12]], [esink[:, :]])
            for l in range(DEPTH):
                for t in range(2):
                    a0 = lamw[:, (l * 4 + 2 * t) * 32:(l * 4 + 2 * t + 1) * 32]
                    a1 = lamw[:, (l * 4 + 2 * t + 1) * 32:(l * 4 + 2 * t + 2) * 32]
                    tmp = tfp[:, 0, 0:32]
                    S.op("dve", lambda e, a0=a0, a1=a1, tmp=tmp: e.tensor_tensor(out=tmp, in0=a0, in1=a1, op=ALU.mult),
                         [a0, a1], [tmp])
                    dst = lams[:, l * 4 + 2 + t:l * 4 + 3 + t]
                    S.op("dve", lambda e, tmp=tmp, dst=dst: e.reduce_sum(out=dst, in_=tmp, axis=AX.X), [tmp], [dst])
                    S.op("act", lambda e, dst=dst: e.activation(out=dst, in_=dst, func=AF.Exp), [dst], [dst])
                e1 = lams[:, l * 4 + 2:l * 4 + 3]; e2 = lams[:, l * 4 + 3:l * 4 + 4]
                lam = lams[:, l * 4:l * 4 + 1]; nlam = lams[:, l * 4 + 1:l * 4 + 2]
                S.op("dve", lambda e, e1=e1, e2=e2, lam=lam, l=l: e.scalar_tensor_tensor(
                    out=lam, in0=e1, scalar=LAM_INIT[l], in1=e2, op0=ALU.add, op1=ALU.subtract), [e1, e2], [lam])
                S.op("dve", lambda e, lam=lam, nlam=nlam: e.tensor_scalar(out=nlam, in0=lam, scalar1=-1.0, scalar2=None, op0=ALU.mult),
                     [lam], [nlam])
                gs = gsub2[:, l:l + 1]; src = pvec[:, PV_GSUB + l:PV_GSUB + l + 1]
                S.op("dve", lambda e, gs=gs, src=src, l=l: e.tensor_scalar(out=gs, in0=src, scalar1=1.0 - LAM_INIT[l], scalar2=None, op0=ALU.mult),
                     [src], [gs])

            mod_tasks = []
            mod_busy = [False]
            S1_hook = [False]

            def mod_panel(l, pn):
                src = w_ada[l, :, pn * 256:(pn + 1) * 256].rearrange("(k p) n -> p k n", p=128)
                wv = W.use(S, src, 8, 256)
                for h in range(2):
                    idx = pn * 2 + h
                    o = ps[:, 7, idx * 2:idx * 2 + 2]
                    for k in range(8):
                        S.op("pe", lambda e, o=o, wv=wv, k=k, h=h: e.matmul(
                            o, lhsT=wv[:, k, h * 128:(h + 1) * 128], rhs=scT[:, k, :], start=(k == 0), stop=(k == 7)),
                            [wv[:, k, h * 128:(h + 1) * 128], scT[:, k, :]], [o])

            def mod_finish(l, v0, v1):
                src = ps[:, 7, 0:96].rearrange("p (i c) -> p i c", c=2)
                for cnd in range(2):
                    o = modT[:, l, v0 * 8:v1 * 8, cnd]; i0 = src[:, v0 * 8:v1 * 8, cnd]
                    i1 = pvec[:, PV_BADA + l * 48 + v0 * 8:PV_BADA + l * 48 + v1 * 8]
                    S.op("dve", lambda e, o=o, i0=i0, i1=i1: e.tensor_tensor(out=o, in0=i0, in1=i1, op=ALU.add), [i0, i1], [o])
                for n in range(2):
                    if not (v0 <= 1 + 3 * n < v1):
                        continue
                    for cnd in range(2):
                        o = amod[:, l, n, cnd, :]
                        sc = modT[:, l, (1 + 3 * n) * 8:(1 + 3 * n) * 8 + 8, cnd]
                        g = pvec[:, PV_GN + (l * 2 + n) * 8:PV_GN + (l * 2 + n) * 8 + 8]
                        S.op("dve", lambda e, o=o, sc=sc, g=g: e.scalar_tensor_tensor(
                            out=o, in0=sc, scalar=1.0, in1=g, op0=ALU.add, op1=ALU.mult), [sc, g], [o])

            def mod_pump(n):
                for _ in range(n):
                    if mod_tasks:
                        mod_tasks.pop(0)()

            def mod_flush():
                while mod_tasks:
                    mod_tasks.pop(0)()

            def mod(l, v, cnd, c):
                return modT[:, l, v * 8 + c, cnd:cnd + 1]

            def rstd_from_sum(pssum, w, scale, slot):
                o = rst[:, slot, 0:w]
                S.op("act", lambda e: e.activation(out=o, in_=pssum, func=AF.Sqrt, scale=scale, bias=epsc[:, 0:1]),
                     [pssum, epsc[:, 0:1]], [o])
                S.op("dve", lambda e: e.reciprocal(out=o, in_=o), [o], [o])
                return o

            def norm_mod(xT, blocks, l, n, cnd, gain_only=None):
                for (c0, c1) in blocks:
                    w = c1 - c0
                    acc = bank(6, w)
                    for c in range(8):
                        sq = tb[:, c % 2, 0:w]
                        xin = xT[:, c, c0:c1]
                        S.op("act", lambda e, sq=sq, xin=xin: e.activation(out=sq, in_=xin, func=AF.Square), [xin], [sq])
                        S.op("pe", lambda e, acc=acc, sq=sq, c=c: e.matmul(acc, lhsT=ONES, rhs=sq, start=(c == 0), stop=(c == 7)),
                             [ONES, sq], [acc])
                    r = rstd_from_sum(acc, w, 1.0 / D, 0)
                    for c in range(8):
                        xin = xT[:, c, c0:c1]
                        if gain_only is None:
                            t = tfp[:, c % 2, 0:w]
                            S.op("dve", lambda e, t=t, xin=xin, r=r: e.tensor_tensor(out=t, in0=xin, in1=r, op=ALU.mult), [xin, r], [t])
                            o = hT[:, c, c0:c1]
                            a = amod[:, l, n, cnd, c:c + 1]; sh = mod(l, 3 * n, cnd, c)
                            S.op("act", lambda e, o=o, t=t, a=a, sh=sh: e.activation(out=o, in_=t, func=AF.Identity, scale=a, bias=sh),
                                 [t, a, sh], [o])
                        else:
                            g = gain_only[:, c:c + 1]
                            S.op("dve", lambda e, xin=xin, r=r, g=g: e.scalar_tensor_tensor(
                                out=xin, in0=xin, scalar=g, in1=r, op0=ALU.mult, op1=ALU.mult), [xin, r, g], [xin])

            def load_x(xd, xT, ntile, col0):
                for g0 in range(0, ntile, 2):
                    for tt in range(2):
                        S.dma("sp", tfp[:, tt, 0:1024], xd[(g0 + tt) * 128:(g0 + tt + 1) * 128, :])
                    for c in range(8):
                        b = ps[:, c % 4, 0:256]
                        for tt in range(2):
                            src = tfp[:, tt, c * 128:(c + 1) * 128]
                            o = ps[:, c % 4, tt * 128:(tt + 1) * 128]
                            S.op("pe", lambda e, o=o, src=src: e.transpose(out=o, in_=src, identity=identF[:, :]),
                                 [src, identF[:, :]], [o])
                        dst = xT[:, c, col0 + g0 * 128:col0 + (g0 + 2) * 128]
                        S.op("dve", lambda e, dst=dst, b=b: e.tensor_copy(out=dst, in_=b), [b], [dst])

            def store_y(xT, yd, ntile, col0):
                for tt in range(ntile):
                    st = tfp[:, tt % 3, 0:1024]
                    for half in range(2):
                        for cc in range(4):
                            c = half * 4 + cc
                            src = xT[:, c, col0 + tt * 128:col0 + (tt + 1) * 128]
                            o = ps[:, half, cc * 128:(cc + 1) * 128]
                            S.op("pe", lambda e, o=o, src=src: e.transpose(out=o, in_=src, identity=identF[:, :]),
                                 [src, identF[:, :]], [o])
                        b = ps[:, half, :]
                        dst = st[:, half * 512:(half + 1) * 512]
                        S.op("dve" if half == 0 else "act",
                             (lambda e, dst=dst, b=b: e.tensor_copy(out=dst, in_=b)) if half == 0 else
                             (lambda e, dst=dst, b=b: e.copy(out=dst, in_=b)), [b], [dst])
                    S.dma("sp", yd[tt * 128:(tt + 1) * 128, :], st)

            def finalize_pair(l, chunk, c0, w, kind, j=0, ob=4):
                for half in range(2):
                    lo = half * 64
                    dp = 64 - lo
                    O = ps[lo:lo + 64, ob + half, 0:w]
                    den = ps[dp:dp + 64, ob + half, 0:w]
                    rd = tfp[lo:lo + 64, 2, (ob // 2 % 2) * 512:(ob // 2 % 2) * 512 + w]
                    if kind == "B":
                        sk = esink[dp:dp + 64, 6 + l * 3 + j:6 + l * 3 + j + 1]
                        S.op("dve", lambda e, rd=rd, den=den, sk=sk: e.tensor_scalar(out=rd, in0=den, scalar1=sk, scalar2=None, op0=ALU.add), [den, sk], [rd])
                        S.op("act", lambda e, rd=rd: e.activation(out=rd, in_=rd, func=AF.Ln), [rd], [rd])
                        S.op("act", lambda e, rd=rd: e.activation(out=rd, in_=rd, func=AF.Exp, scale=-1.0), [rd], [rd])
                    else:
                        S.op("act", lambda e, rd=rd, den=den: e.activation(out=rd, in_=den, func=AF.Ln), [den], [rd])
                        S.op("act", lambda e, rd=rd: e.activation(out=rd, in_=rd, func=AF.Exp, scale=-1.0), [rd], [rd])
                    o = hT[lo:lo + 64, chunk, c0:c0 + w]
                    S.op("dve", lambda e, o=o, O=O, rd=rd: e.tensor_tensor(out=o, in0=O, in1=rd, op=ALU.mult), [O, rd], [o])

            obank = [4]

            def dense_pair(l, qchunk, kfn, vfn, nkt, qc0, w, scale, ochunk, oc0, kind, j=0):
                if mod_busy[0]:
                    ob = 4
                else:
                    ob = obank[0]
                    obank[0] = 10 - ob

                def emit_s(kt):
                    slot = kt % 2
                    for half in range(2):
                        lo = half * 64
                        o = ps[:, slot * 2 + half, 0:w]
                        kk = kfn(kt)[lo:lo + 64, :]
                        qq = qk[lo:lo + 64, qchunk, qc0:qc0 + w]
                        S.op("pe", lambda e, o=o, kk=kk, qq=qq: e.matmul(o, lhsT=kk, rhs=qq, start=True, stop=True), [kk, qq], [o])

                def emit_exp(kt):
                    slot = kt % 2
                    pt = PT[kt % 3]
                    src = ps[:, slot * 2:slot * 2 + 2, 0:w]
                    dst = pt[:, :].rearrange("p (a b) -> p a b", a=2)[:, :, 0:w]
                    S.op("act", lambda e, dst=dst, src=src: e.activation(out=dst, in_=src, func=AF.Exp, scale=scale), [src], [dst])

                def emit_pv(kt):
                    pt = PT[kt % 3]
                    for half in range(2):
                        o = ps[:, ob + half, 0:w]
                        vv = vfn(kt)[:, half * 64:half * 64 + 128]
                        pp = pt[:, half * 512:half * 512 + w]
                        S.op("pe", lambda e, o=o, vv=vv, pp=pp, kt=kt: e.matmul(o, lhsT=vv, rhs=pp, start=(kt == 0), stop=(kt == nkt - 1)),
                             [vv, pp], [o])

                emit_s(0)
                for kt in range(nkt):
                    if kt + 1 < nkt:
                        emit_s(kt + 1)
                    emit_exp(kt)
                    emit_pv(kt)
                finalize_pair(l, ochunk, oc0, w, kind, j, ob)
                mod_pump(1)

            def diff_head(l, h, qfn, kfn, vfn, nkt, w, scale, oc_rows):
                hh = h % 2
                lo = hh * 64

                def emit_s(kt):
                    slot = kt % 2
                    for m in range(2):
                        o = ps[:, slot * 2 + m, 0:w]
                        if hh == 1 and m == 1:
                            kk = kfn(kt)[64:128, :]; qq = qfn(True)[64:128, :]
                        else:
                            r0 = lo + 32 * m
                            kk = kfn(kt)[r0:r0 + 32, :]; qq = qfn(False)[r0:r0 + 32, :]
                        S.op("pe", lambda e, o=o, kk=kk, qq=qq: e.matmul(o, lhsT=kk, rhs=qq, start=True, stop=True), [kk, qq], [o])

                def emit_exp(kt):
                    slot = kt % 2
                    pt = PT[kt % 3]
                    src = ps[:, slot * 2:slot * 2 + 2, 0:w]
                    dst = pt[:, :].rearrange("p (a b) -> p a b", a=2)[:, :, 0:w]
                    S.op("act", lambda e, dst=dst, src=src: e.activation(out=dst, in_=src, func=AF.Exp, scale=scale), [src], [dst])

                def emit_pv(kt):
                    pt = PT[kt % 3]
                    for m in range(2):
                        o = ps[:, 4 + m, 0:w]
                        vv = vfn(kt)[:, hh * 64:hh * 64 + 128]
                        pp = pt[:, m * 512:m * 512 + w]
                        S.op("pe", lambda e, o=o, vv=vv, pp=pp, kt=kt: e.matmul(o, lhsT=vv, rhs=pp, start=(kt == 0), stop=(kt == nkt - 1)),
                             [vv, pp], [o])

                emit_s(0)
                for kt in range(nkt):
                    if kt + 1 < nkt:
                        emit_s(kt + 1)
                    emit_exp(kt)
                    emit_pv(kt)
                dp = 64 - lo
                res = []
                for m in range(2):
                    O = ps[lo:lo + 64, 4 + m, 0:w]; den = ps[dp:dp + 64, 4 + m, 0:w]
                    rd = tfp[lo:lo + 64, 2, m * 512:m * 512 + w]
                    S.op("act", lambda e, rd=rd, den=den: e.activation(out=rd, in_=den, func=AF.Ln), [den], [rd])
                    S.op("act", lambda e, rd=rd: e.activation(out=rd, in_=rd, func=AF.Exp, scale=-1.0), [rd], [rd])
                    S.op("dve", lambda e, rd=rd, O=O: e.tensor_tensor(out=rd, in0=O, in1=rd, op=ALU.mult), [O, rd], [rd])
                    res.append(rd)
                mod_pump(3)
                o = ocT[lo:lo + 64, 0:w]
                nl = lams[lo:lo + 64, l * 4 + 1:l * 4 + 2]
                S.op("dve", lambda e, o=o, a=res[0], b=res[1], nl=nl: e.scalar_tensor_tensor(
                    out=o, in0=b, scalar=nl, in1=a, op0=ALU.mult, op1=ALU.add), [res[0], res[1], nl], [o])

            def subnorm_pair(l, ochunk, oc0, w):
                sq = tb[:, 2, 0:w]
                S.op("act", lambda e: e.activation(out=sq, in_=ocT[:, 0:w], func=AF.Square), [ocT[:, 0:w]], [sq])
                acc = bank(6, w)
                S.op("pe", lambda e: e.matmul(acc, lhsT=BD64, rhs=sq, start=True, stop=True), [BD64, sq], [acc])
                r = rstd_from_sum(acc, w, 1.0 / 64, 1)
                o = hT[:, ochunk, oc0:oc0 + w]
                gs = gsub2[:, l:l + 1]
                S.op("dve", lambda e: e.scalar_tensor_tensor(out=o, in0=ocT[:, 0:w], scalar=gs, in1=r, op0=ALU.mult, op1=ALU.mult),
                     [ocT[:, 0:w], gs, r], [o])

            def qk_post(l, cc, z, c0, w, rope_on):
                o = qk[:, cc, c0:c0 + w]
                isA = cc in (CK_AK, CK_A0, CK_A0 + 1, CK_A0 + 2)
                isC = cc in (CK_CK0, CK_CK1, CK_C0, CK_C0 + 1)
                if isA:
                    g = pvec[:, PV_GQK + l * 2 + (1 if cc == CK_AK else 0):PV_GQK + l * 2 + (1 if cc == CK_AK else 0) + 1]
                    sq = tb[:, 2, 0:w]
                    S.op("act", lambda e: e.activation(out=sq, in_=z, func=AF.Square), [z], [sq])
                    acc = bank(6, w)
                    S.op("pe", lambda e: e.matmul(acc, lhsT=BD64, rhs=sq, start=True, stop=True), [BD64, sq], [acc])
                    r = rstd_from_sum(acc, w, 1.0 / 64, 1)
                    if not rope_on:
                        S.op("dve", lambda e: e.scalar_tensor_tensor(out=o, in0=z, scalar=g, in1=r, op0=ALU.mult, op1=ALU.mult), [z, g, r], [o])
                        return
                    zg = tb[:, 0, 0:w]
                    S.op("dve", lambda e: e.tensor_scalar(out=zg, in0=z, scalar1=g, scalar2=None, op0=ALU.mult), [z, g], [zg])
                else:
                    if not rope_on:
                        S.op("act", lambda e: e.copy(out=o, in_=z), [z], [o])
                        return
                    zg = tb[:, 0, 0:w]
                    S.op("act", lambda e: e.copy(out=zg, in_=z), [z], [zg])
                    r = None
                Rm = R32 if isC else R64
                ti = 2 if isC else 0
                cs = rope[:, ti, c0 - 1:c0 - 1 + w]; sn = rope[:, ti + 1, c0 - 1:c0 - 1 + w]
                rot = bank(7, w)
                S.op("pe", lambda e: e.matmul(rot, lhsT=Rm, rhs=zg, start=True, stop=True), [Rm, zg], [rot])
                t1 = tfp[:, 0, 0:w]; t2 = tfp[:, 1, 0:w]
                S.op("dve", lambda e: e.tensor_tensor(out=t1, in0=zg, in1=cs, op=ALU.mult), [zg, cs], [t1])
                S.op("dve", lambda e: e.tensor_tensor(out=t2, in0=rot, in1=sn, op=ALU.mult), [rot, sn], [t2])
                if r is None:
                    S.op("dve", lambda e: e.tensor_tensor(out=o, in0=t1, in1=t2, op=ALU.add), [t1, t2], [o])
                else:
                    S.op("dve", lambda e: e.tensor_tensor(out=t1, in0=t1, in1=t2, op=ALU.add), [t1, t2], [t1])
                    S.op("dve", lambda e: e.tensor_tensor(out=o, in0=t1, in1=r, op=ALU.mult), [t1, r], [o])

            def inproj_qk(l, panel, blocks, rope_on, bb):
                src = w_inp[l, :, panel * 256:(panel + 1) * 256].rearrange("(k p) n -> p k n", p=128)
                wv = W.use(S, src, 8, 256)
                for h in range(2):
                    cc = panel * 2 + h
                    for (c0, c1) in blocks:
                        w = c1 - c0
                        z = bank(bb[0] % 4, w)
                        bb[0] += 1
                        for k in range(8):
                            lh = wv[:, k, h * 128:(h + 1) * 128]; rh = hT[:, k, c0:c1]
                            S.op("pe", lambda e, z=z, lh=lh, rh=rh, k=k: e.matmul(z, lhsT=lh, rhs=rh, start=(k == 0), stop=(k == 7)), [lh, rh], [z])
                        qk_post(l, cc, z, c0, w, rope_on)

            def outproj(l, xT, blocks, cnd):
                mod_flush()
                bbi = 0
                for pn in range(4):
                    src = w_outp[l, :, pn * 256:(pn + 1) * 256].rearrange("(k p) n -> p k n", p=128)
                    wv = W.use(S, src, 8, 256)
                    for h in range(2):
                        m = pn * 2 + h
                        for (c0, c1) in blocks:
                            w = c1 - c0
                            y = bank(bbi % 4, w); bbi += 1
                            for k in range(8):
                                lh = wv[:, k, h * 128:(h + 1) * 128]; rh = hT[:, k, c0:c1]
                                S.op("pe", lambda e, y=y, lh=lh, rh=rh, k=k: e.matmul(y, lhsT=lh, rhs=rh, start=(k == 0), stop=(k == 7)), [lh, rh], [y])
                            xx = xT[:, m, c0:c1]; gt = mod(l, 2, cnd, m)
                            S.op("dve", lambda e, xx=xx, y=y, gt=gt: e.scalar_tensor_tensor(out=xx, in0=y, scalar=gt, in1=xx, op0=ALU.mult, op1=ALU.add),
                                 [y, gt, xx], [xx])

            def ffn(l, xT, cnd, is_s):
                if is_s:
                    ublocks = [(0, 512), (512, 1024), (1024, 1026)]
                    ncol = 1026
                    dblocks = [(1, 513), (513, 1025)]
                else:
                    ublocks = [(0, 512)]
                    ncol = 512
                    dblocks = [(0, 512)]
                for g in range(2):
                    for jj in range(11):
                        j = g * 11 + jj
                        src = w_upp[l, :, j, :].rearrange("(k p) n -> p k n", p=128)
                        wv = W.use(S, src, 8, 256)
                        cres = []
                        for ag in range(2):
                            b0 = ag * 3
                            u = ps[:, b0:b0 + 3, :].rearrange("p a b -> p (a b)")
                            for (c0, c1) in ublocks:
                                for k in range(8):
                                    lh = wv[:, k, ag * 128:(ag + 1) * 128]; rh = hT[:, k, c0:c1]; o = u[:, c0:c1]
                                    S.op("pe", lambda e, o=o, lh=lh, rh=rh, k=k: e.matmul(o, lhsT=lh, rhs=rh, start=(k == 0), stop=(k == 7)), [lh, rh], [o])
                            base = PV_CW + ((l * NJ + j) * 2 + ag) * 4
                            w0 = pvec[:, base:base + 1]; w1 = pvec[:, base + 1:base + 2]; w2 = pvec[:, base + 2:base + 3]; cb = pvec[:, base + 3:base + 4]
                            c = tfp[:, ag, 0:ncol]
                            S.op("act", lambda e, c=c, u=u, w1=w1, cb=cb: e.activation(out=c, in_=u[:, 0:ncol], func=AF.Identity, scale=w1, bias=cb),
                                 [u[:, 0:ncol], w1, cb], [c])
                            if is_s:
                                a1 = c[:, 1:1025]
                                S.op("dve", lambda e, a1=a1, u=u, w0=w0: e.scalar_tensor_tensor(out=a1, in0=u[:, 0:1024], scalar=w0, in1=a1, op0=ALU.mult, op1=ALU.add),
                                     [u[:, 0:1024], w0, a1], [a1])
                                S.op("dve", lambda e, a1=a1, u=u, w2=w2: e.scalar_tensor_tensor(out=a1, in0=u[:, 2:1026], scalar=w2, in1=a1, op0=ALU.mult, op1=ALU.add),
                                     [u[:, 2:1026], w2, a1], [a1])
                            else:
                                c3 = c.rearrange("p (b t) -> p b t", b=2); u3 = u[:, 0:512].rearrange("p (b t) -> p b t", b=2)
                                S.op("dve", lambda e, c3=c3, u3=u3, w0=w0: e.scalar_tensor_tensor(out=c3[:, :, 1:256], in0=u3[:, :, 0:255], scalar=w0, in1=c3[:, :, 1:256], op0=ALU.mult, op1=ALU.add),
                                     [u[:, 0:512], w0, c], [c])
                                S.op("dve", lambda e, c3=c3, u3=u3, w2=w2: e.scalar_tensor_tensor(out=c3[:, :, 0:255], in0=u3[:, :, 1:256], scalar=w2, in1=c3[:, :, 0:255], op0=ALU.mult, op1=ALU.add),
                                     [u[:, 0:512], w2, c], [c])
                            cres.append(c)
                        sa = tb[:, 0, 0:ncol]
                        S.op("act", lambda e, sa=sa, c=cres[0]: e.activation(out=sa, in_=c, func=AF.Silu), [cres[0]], [sa])
                        dst = qk[:, jj, 0:ncol]
                        S.op("pool", lambda e, dst=dst, sa=sa, cg=cres[1]: e.tensor_tensor(out=dst, in0=sa, in1=cg, op=ALU.mult), [sa, cres[1]], [dst])
                    bbi = 0
                    for m in range(8):
                        src = w_downp[l, g * 11 * 128:(g + 1) * 11 * 128, m * 128:(m + 1) * 128].rearrange("(k p) n -> p k n", p=128)
                        wv = W.use(S, src, 11, 128)
                        for (c0, c1) in dblocks:
                            w = c1 - c0
                            y = bank(6 + bbi % 2, w); bbi += 1
                            for k in range(11):
                                lh = wv[:, k, :]; rh = qk[:, k, c0:c1]
                                S.op("pe", lambda e, y=y, lh=lh, rh=rh, k=k: e.matmul(y, lhsT=lh, rhs=rh, start=(k == 0), stop=(k == 10)), [lh, rh], [y])
                            xx = xT[:, m, c0:c1]; gt = mod(l, 5, cnd, m)
                            S.op("dve", lambda e, xx=xx, y=y, gt=gt: e.scalar_tensor_tensor(out=xx, in0=y, scalar=gt, in1=xx, op0=ALU.mult, op1=ALU.add),
                                 [y, gt, xx], [xx])

            PB = [(0, 512)]

            def layer_P(l):
                norm_mod(xTp, PB, l, 0, 0)
                bb = [0]
                for panel in range(6):
                    inproj_qk(l, panel, PB, False, bb)
                vP = vloc[:, 0:3072].rearrange("p (t f) -> p t f", f=768)
                S.op("pool", lambda e: e.memset(vP[:, :, 64:128], 1.0), [], [vP[:, :, 64:128]])
                S.op("pool", lambda e: e.memset(vP[:, :, 256:320], 1.0), [], [vP[:, :, 256:320]])
                S.op("pool", lambda e: e.memset(vP[:, :, 448:512], 1.0), [], [vP[:, :, 448:512]])
                S.op("pool", lambda e: e.memset(vP[:, :, 640:704], 1.0), [], [vP[:, :, 640:704]])
                wvs = []
                for pn in range(6, 10):
                    src = w_inp[l, :, pn * 256:(pn + 1) * 256].rearrange("(k p) n -> p k n", p=128)
                    wvs.append(W.use(S, src, 8, 256, look=9 - pn))
                for tt in range(4):
                    kv = tfp[:, tt % 3, 0:1024]
                    for pi in range(4):
                        z = ps[:, pi, 0:256]
                        for k in range(8):
                            lh = hT[:, k, tt * 128:(tt + 1) * 128]; rh = wvs[pi][:, k, :]
                            S.op("pe", lambda e, z=z, lh=lh, rh=rh, k=k: e.matmul(z, lhsT=lh, rhs=rh, start=(k == 0), stop=(k == 7)), [lh, rh], [z])
                        dst = kv[:, pi * 256:(pi + 1) * 256]
                        if pi % 2 == 0:
                            S.op("dve", lambda e, dst=dst, z=z: e.tensor_copy(out=dst, in_=z), [z], [dst])
                        else:
                            S.op("act", lambda e, dst=dst, z=z: e.copy(out=dst, in_=z), [z], [dst])
                    for g in range(2):
                        kag = kv[:, 512 + g * 64:512 + (g + 1) * 64]
                        sq = tb[:, 0, g * 64:(g + 1) * 64]
                        S.op("dve", lambda e, sq=sq, kag=kag: e.tensor_tensor(out=sq, in0=kag, in1=kag, op=ALU.mult), [kag], [sq])
                        ss = sst[:, g:g + 1]
                        S.op("dve", lambda e, ss=ss, sq=sq: e.reduce_sum(out=ss, in_=sq, axis=AX.X), [sq], [ss])
                        S.op("act", lambda e, ss=ss: e.activation(out=ss, in_=ss, func=AF.Sqrt, scale=1.0 / 64, bias=epsc[:, 0:1]), [ss, epsc[:, 0:1]], [ss])
                        S.op("dve", lambda e, ss=ss: e.reciprocal(out=ss, in_=ss), [ss], [ss])
                        S.op("dve", lambda e, kag=kag, ss=ss, l=l: e.scalar_tensor_tensor(out=kag, in0=kag, scalar=ss, in1=gkar[:, l, :], op0=ALU.mult, op1=ALU.mult),
                             [kag, ss, gkar[:, l, :]], [kag])
                    vt = vP[:, tt, :]
                    for mx in range(2):
                        o = vt[:, mx * 192:mx * 192 + 256].rearrange("p (g x) -> p g x", x=128)[:, :, 0:64]
                        i = kv[:, mx * 128:(mx + 1) * 128].rearrange("p (g x) -> p g x", x=64)
                        S.op("pool", lambda e, o=o, i=i: e.tensor_copy(out=o, in_=i), [kv[:, mx * 128:(mx + 1) * 128]], [vt[:, mx * 192:mx * 192 + 192]])
                    vc3 = vt[:, 384:768].rearrange("p (a r) -> p a r", a=2)
                    i4 = kv[:, 256:512].rearrange("p (a g x) -> p a g x", a=2, g=2)
                    for g in range(2):
                        o = vc3[:, :, g * 128:g * 128 + 64]; i = i4[:, :, g, :]
                        S.op("pool", lambda e, o=o, i=i: e.tensor_copy(out=o, in_=i), [kv[:, 256:512]], [vt[:, 384:768]])
                    bi, t0 = tt // 2, (tt % 2) * 128
                    S.dma("sp", o_av[bi, l, t0:t0 + 128, :], kv[:, 0:128])
                    S.dma("sp", o_bv[bi, l, t0:t0 + 128, :], kv[:, 128:256])
                    S.dma("sp", o_cv[bi, l, t0:t0 + 128, :], kv[:, 256:512])
                    S.dma("sp", o_ak[bi, l, t0:t0 + 128, :], kv[:, 512:640])
                    S.dma("sp", o_bk[bi, l, t0:t0 + 128, :], kv[:, 640:768])
                    S.dma("sp", o_ck[bi, l, t0:t0 + 128, :], kv[:, 768:1024])
                for bi in range(2):
                    q0 = bi * 256
                    for j in range(3):
                        dense_pair(l, CK_A0 + j, lambda kt: qk[:, CK_AK, q0 + kt * 128:q0 + (kt + 1) * 128],
                                   lambda kt: vP[:, bi * 2 + kt, 0:192], 2, q0, 256, 0.125, j, q0, "A")
                    for j in range(3):
                        dense_pair(l, CK_B0 + j, lambda kt: qk[:, CK_BK, q0 + kt * 128:q0 + (kt + 1) * 128],
                                   lambda kt: vP[:, bi * 2 + kt, 192:384], 2, q0, 256, 0.125, 3 + j, q0, "B", j)
                    for cp in range(2):
                        src = qk[64:128, CK_C0 + cp, q0:q0 + 256]
                        S.op("pool", lambda e, src=src, cp=cp: e.tensor_copy(out=qm[64:128, cp, 0:256], in_=src), [src], [qm[64:128, cp, 0:256]])
                        S.op("pool", lambda e, cp=cp: e.memset(qm[64:96, cp, 0:256], 0.0), [], [qm[64:96, cp, 0:256]])
                        for hh in range(2):
                            h = cp * 2 + hh
                            diff_head(l, h,
                                      lambda masked, cp=cp: (qm[:, cp, 0:256] if masked else qk[:, CK_C0 + cp, q0:q0 + 256]),
                                      lambda kt, cp=cp: qk[:, CK_CK0 + cp, q0 + kt * 128:q0 + (kt + 1) * 128],
                                      lambda kt, cp=cp: vP[:, bi * 2 + kt, 384 + cp * 192:384 + (cp + 1) * 192],
                                      2, 256, 32 ** -0.5, None)
                        subnorm_pair(l, 6 + cp, q0, 256)
                outproj(l, xTp, PB, 0)
                norm_mod(xTp, PB, l, 1, 0)
                ffn(l, xTp, 0, False)


            SBK = [(1, 513), (513, 1025)]
            vBf = vB[:, :, :].rearrange("p t f -> p (t f)")

            def vb_pair_view(tile):
                return vBf[:, tile * 192:tile * 192 + 256].rearrange("p (g x) -> p g x", x=128)[:, :, 0:64]

            def load_ctxT(src_dT, dst, slot):
                S.dma("sp", tfp[:, slot, 0:256], src_dT)
                S.op("dve", lambda e: e.tensor_copy(out=dst, in_=tfp[:, slot, 0:256]), [tfp[:, slot, 0:256]], [dst])

            def layer_S(l):
                if DBG == 10:
                    return
                agi = aginK[l].ap(); ago = agoutK[l].ap()
                agiVr = [aginV[l][i].ap() for i in range(2)]; agoVr = [agoutV[l][i].ap() for i in range(2)]
                agiV = [a.rearrange("r c -> (r c)").rearrange("(t f) -> t f", f=384) for a in agiVr]
                agoVd = [a.rearrange("r c -> (r c)").rearrange("(t f) -> t f", f=384) for a in agoVr]
                norm_mod(xTs, SBK, l, 0, 1)
                bb = [0]
                inproj_qk(l, 0, SBK, True, bb)
                inproj_qk(l, 1, SBK, True, bb)
                for cc in range(4):
                    S.dma("sp", agi[cc * 128:(cc + 1) * 128, :], qk[:, cc, 1:1025])
                S.collective(lambda e: e.collective_compute("AllGather", ALU.bypass, replica_groups=RG, ins=[agi], outs=[ago]),
                             [agi], [ago])
                if DBG == 12:
                    return
                wvs = []
                for pn in (6, 7):
                    src = w_inp[l, :, pn * 256:(pn + 1) * 256].rearrange("(k p) n -> p k n", p=128)
                    wvs.append(W.use(S, src, 8, 256, look=7 - pn))
                vS = vloc[:, 0:3072].rearrange("p (t f) -> p t f", f=768)
                for cbase in (64, 256, 448, 640):
                    S.op("pool", lambda e, cbase=cbase: e.memset(vS[:, :, cbase:cbase + 64], 1.0), [], [vS[:, :, cbase:cbase + 64]])
                for tt in range(8):
                    sl = tt % 4
                    zz = []
                    for pi in range(2):
                        z = ps[:, (tt % 2) * 2 + pi, 0:256]
                        zz.append(z)
                        for k in range(8):
                            lh = hT[:, k, 1 + tt * 128:1 + (tt + 1) * 128]; rh = wvs[pi][:, k, :]
                            S.op("pe", lambda e, z=z, lh=lh, rh=rh, k=k: e.matmul(z, lhsT=lh, rhs=rh, start=(k == 0), stop=(k == 7)), [lh, rh], [z])
                    vt = vS[:, sl, :]
                    for mx in range(2):
                        o = vt[:, mx * 192:mx * 192 + 256].rearrange("p (g x) -> p g x", x=128)[:, :, 0:64]
                        i = zz[0][:, mx * 128:(mx + 1) * 128].rearrange("p (g x) -> p g x", x=64)
                        if mx == 0:
                            S.op("dve", lambda e, o=o, i=i: e.tensor_copy(out=o, in_=i), [zz[0][:, mx * 128:(mx + 1) * 128]], [vt[:, mx * 192:mx * 192 + 192]])
                        else:
                            S.op("dve", lambda e, o=o, i=i: e.tensor_copy(out=o, in_=i), [zz[0][:, mx * 128:(mx + 1) * 128]], [vt[:, mx * 192:mx * 192 + 192]])
                    vc3 = vt[:, 384:768].rearrange("p (a r) -> p a r", a=2)
                    i4 = zz[1].rearrange("p (a g x) -> p a g x", a=2, g=2)
                    for g in range(2):
                        o = vc3[:, :, g * 128:g * 128 + 64]; i = i4[:, :, g, :]
                        if g == 0:
                            S.op("dve", lambda e, o=o, i=i: e.tensor_copy(out=o, in_=i), [zz[1]], [vt[:, 384:768]])
                        else:
                            S.op("dve", lambda e, o=o, i=i: e.tensor_copy(out=o, in_=i), [zz[1]], [vt[:, 384:768]])
                    if DBG != 131:
                        S.dma("sp", agiV[0][tt * 128:(tt + 1) * 128, :], vt[:, 0:384])
                        S.dma("sp", agiV[1][tt * 128:(tt + 1) * 128, :], vt[:, 384:768])
                    if DBG != 132:
                        S.op("pool", lambda e, tt=tt, vt=vt: e.tensor_copy(out=vB[:, 3 + tt, :], in_=vt[:, 192:384]), [vt[:, 192:384]], [vB[:, 3 + tt, :]])
                if DBG in (13, 131, 132):
                    return
                for i in range(2):
                    S.collective(lambda e, i=i: e.collective_compute("AllGather", ALU.bypass, replica_groups=RG, ins=[agiVr[i]], outs=[agoVr[i]]),
                                 [agiVr[i]], [agoVr[i]])
                if DBG == 1:
                    return
                for panel in range(2, 6):
                    inproj_qk(l, panel, SBK, True, bb)
                if DBG == 2:
                    return
                if DBG == 22:
                    S.op("dve", lambda e: e.tensor_copy(out=gK[:, 0, 0:128], in_=kBx[:, 1, :]), [kBx[:, 1, :]], [gK[:, 0, 0:128]])
                    return
                if DBG == 23:
                    S.op("pe", lambda e: e.matmul(ps[:, 7, 0:128], lhsT=tb[:, 0, 0:128], rhs=identB[:, :], start=True, stop=True), [tb[:, 0, 0:128], identB[:, :]], [ps[:, 7, 0:128]])
                    S.op("dve", lambda e: e.tensor_copy(out=kBx[:, 0, :], in_=ps[:, 7, 0:128]), [ps[:, 7, 0:128]], [kBx[:, 0, :]])
                    return
                if DBG == 21:
                    for _i in range(8):
                        S.op("dve", lambda e: e.tensor_copy(out=kBx[:, 0, :], in_=kBx[:, 1, :]), [kBx[:, 1, :]], [kBx[:, 0, :]])
                    return
                load_ctxT(cak[l], gK[:, 0, 0:256], 0)
                load_ctxT(cbk[l], kBx[:, 0:2, :].rearrange("p t x -> p (t x)"), 1)
                if DBG in (31, 311, 313, 314):
                    return

                def ctx_v(src_d, dst_fn):
                    for t in range(2):
                        S.dma("sp", tfp[:, t, 0:128], src_d[t * 128:(t + 1) * 128, :])
                        S.op("pool", lambda e, t=t: e.tensor_copy(out=dst_fn(t), in_=tfp[:, t, 0:128].rearrange("p (g x) -> p g x", x=64)),
                             [tfp[:, t, 0:128]], [dst_fn(t)])

                gVf = gV[:, :, :].rearrange("p t f -> p (t f)")
                ctx_v(cav[l], lambda t: gVf[:, t * 192:t * 192 + 256].rearrange("p (g x) -> p g x", x=128)[:, :, 0:64])
                ctx_v(cbv[l], lambda t: vb_pair_view(t))
                if DBG == 32:
                    return
                for rr in range(4):
                    S.dma("sp", gK[:, 0, 256 + rr * 1024:256 + (rr + 1) * 1024], ago[rr * 512:rr * 512 + 128, :])
                    S.dma("sp", gV[:, 2 + rr * 8:10 + rr * 8, :],
                          agoVd[0][rr * 1024:(rr + 1) * 1024, 0:192].rearrange("(t p) f -> p t f", p=128))
                if DBG == 3:
                    return
                candK = tb[:, 1, 0:1024].rearrange("p (r e x) -> p r e x", r=4, e=2)
                candVs = [tb[:, 2, 0:768].rearrange("p (r x) -> p r x", r=4), tb[:, 0, 0:768].rearrange("p (r x) -> p r x", r=4)]
                for rr in range(4):
                    S.dma("sp", candK[:, rr, 0, :], ago[rr * 512 + 128:rr * 512 + 256, 0:128])
                    S.dma("sp", candK[:, rr, 1, :], ago[rr * 512 + 128:rr * 512 + 256, 896:1024])
                    S.dma("sp", candVs[0][:, rr, :], agoVd[0][rr * 1024 + 896:rr * 1024 + 1024, 192:384])
                    S.dma("sp", candVs[1][:, rr, :], agoVd[0][rr * 1024:rr * 1024 + 128, 192:384])
                for side in range(2):
                    e_idx = 1 - side
                    kd = kBx[:, 2 + side, :]
                    vd = vB[:, 2 if side == 0 else 11, :]
                    for rr in range(4):
                        sc = sel[:, side * 4 + rr:side * 4 + rr + 1]
                        ck = candK[:, rr, e_idx, :]
                        cv = candVs[side][:, rr, :]
                        if rr == 0:
                            S.op("dve", lambda e, kd=kd, ck=ck, sc=sc: e.tensor_scalar(out=kd, in0=ck, scalar1=sc, scalar2=None, op0=ALU.mult), [ck, sc], [kd])
                            S.op("dve", lambda e, vd=vd, cv=cv, sc=sc: e.tensor_scalar(out=vd, in0=cv, scalar1=sc, scalar2=None, op0=ALU.mult), [cv, sc], [vd])
                        else:
                            S.op("dve", lambda e, kd=kd, ck=ck, sc=sc: e.scalar_tensor_tensor(out=kd, in0=ck, scalar=sc, in1=kd, op0=ALU.mult, op1=ALU.add), [ck, sc, kd], [kd])
                            S.op("dve", lambda e, vd=vd, cv=cv, sc=sc: e.scalar_tensor_tensor(out=vd, in0=cv, scalar=sc, in1=vd, op0=ALU.mult, op1=ALU.add), [cv, sc, vd], [vd])
                if DBG == 4:
                    return
                for qb in range(2):
                    c0 = 1 + qb * 512
                    for j in range(3):
                        dense_pair(l, CK_A0 + j, lambda kt: gK[:, 0, kt * 128:(kt + 1) * 128], lambda kt: gV[:, kt, :],
                                   34, c0, 512, 0.125, j, c0, "A")
                def load_c(cp, kc):
                    load_ctxT(cck[l][cp * 128:(cp + 1) * 128, :], gK[:, kc, 0:256], 0)
                    for rr in range(4):
                        S.dma("sp", gK[:, kc, 256 + rr * 1024:256 + (rr + 1) * 1024], ago[rr * 512 + 256 + cp * 128:rr * 512 + 384 + cp * 128, :])
                load_c(0, 1)
                if DBG == 5:
                    return
                def bkey(t):
                    if t < 0:
                        return kBx[:, 2, :]
                    if t > 7:
                        return kBx[:, 3, :]
                    return qk[:, CK_BK, 1 + t * 128:1 + (t + 1) * 128]
                tbf = tb[:, :, :].rearrange("p t f -> p (t f)")
                steps = [(qb4, j, n) for qb4 in range(2) for j in range(3) for n in range(4)]

                def b_geo(si):
                    qb4, j, n = steps[si]
                    nn = qb4 * 4 + n
                    base = 0 if si % 2 == 0 else 5
                    sps = ps[:, base:base + 3, :].rearrange("p a b -> p (a b)")
                    pt = tbf[:, (si % 2) * 1280:(si % 2 + 1) * 1280]
                    return qb4, j, n, nn, sps, pt

                def b_emit_s(si):
                    qb4, j, n, nn, sps, pt = b_geo(si)
                    c0 = 1 + nn * 128
                    keys = [kBx[:, 0, :], kBx[:, 1, :], bkey(nn - 1), bkey(nn), bkey(nn + 1)]
                    for half in range(2):
                        lo = half * 64
                        for i5 in range(5):
                            o = sps[:, half * 640 + i5 * 128:half * 640 + (i5 + 1) * 128]
                            masked = i5 in (2, 4)
                            kk = keys[i5][lo:lo + 64, :]; qq = qk[lo:lo + 64, CK_B0 + j, c0:c0 + 128]
                            S.op("pe", lambda e, o=o, kk=kk, qq=qq, masked=masked: e.matmul(o, lhsT=kk, rhs=qq, start=True, stop=(not masked)), [kk, qq], [o])
                            if masked:
                                mb = bmask[:, (2 if nn == 0 else 0), :] if i5 == 2 else bmask[:, (3 if nn == 7 else 1), :]
                                S.op("pe", lambda e, o=o, mb=mb: e.matmul(o, lhsT=identB[:, :], rhs=mb, start=False, stop=True), [identB[:, :], mb], [o])

                def b_emit_exp(si):
                    qb4, j, n, nn, sps, pt = b_geo(si)
                    S.op("act", lambda e: e.activation(out=pt, in_=sps[:, 0:1280], func=AF.Exp, scale=0.125), [sps[:, 0:1280]], [pt])

                def b_emit_pv(si):
                    qb4, j, n, nn, sps, pt = b_geo(si)
                    vts = [0, 1, 2 + nn, 3 + nn, 4 + nn]
                    for half in range(2):
                        for i5 in range(5):
                            o = ps[:, 3 + half, n * 128:(n + 1) * 128]
                            vv = vB[:, vts[i5], half * 64:half * 64 + 128]; pp = pt[:, half * 640 + i5 * 128:half * 640 + (i5 + 1) * 128]
                            S.op("pe", lambda e, o=o, vv=vv, pp=pp, i5=i5: e.matmul(o, lhsT=vv, rhs=pp, start=(i5 == 0), stop=(i5 == 4)), [vv, pp], [o])
                    if n == 3:
                        finalize_pair(l, 3 + j, 1 + qb4 * 512, 512, "B", j, ob=3)

                b_emit_s(0)
                for si in range(len(steps)):
                    if si + 1 < len(steps):
                        b_emit_s(si + 1)
                    b_emit_exp(si)
                    b_emit_pv(si)
                if S1_hook[0] and l == 0:
                    S1_hook[0] = False
                    mod_busy[0] = True
                    for pn in range(24):
                        mod_tasks.append(lambda pn=pn: mod_panel(1, pn))

                    def _fin1():
                        mod_finish(1, 0, 6)
                        mod_busy[0] = False
                    mod_tasks.append(_fin1)
                for cp in range(2):
                    kc = 1 - cp
                    if cp == 1:
                        load_c(1, 0)
                    ctx_v(ccv[l][:, cp * 128:(cp + 1) * 128], lambda t: gVf[:, t * 192:t * 192 + 256].rearrange("p (g x) -> p g x", x=128)[:, :, 0:64])
                    for rr in range(4):
                        S.dma("sp", gV[:, 2 + rr * 8:10 + rr * 8, :],
                              agoVd[1][rr * 1024:(rr + 1) * 1024, cp * 192:(cp + 1) * 192].rearrange("(t p) f -> p t f", p=128))
                    for qb in range(2):
                        c0 = 1 + qb * 512
                        src = qk[64:128, CK_C0 + cp, c0:c0 + 512]
                        S.op("pool", lambda e, src=src, cp=cp: e.tensor_copy(out=qm[64:128, cp, :], in_=src), [src], [qm[64:128, cp, :]])
                        S.op("pool", lambda e, cp=cp: e.memset(qm[64:96, cp, :], 0.0), [], [qm[64:96, cp, :]])
                        for hh in range(2):
                            diff_head(l, cp * 2 + hh,
                                      lambda masked, cp=cp, c0=c0: (qm[:, cp, :] if masked else qk[:, CK_C0 + cp, c0:c0 + 512]),
                                      lambda kt, kc=kc: gK[:, kc, kt * 128:(kt + 1) * 128],
                                      lambda kt: gV[:, kt, :], 34, 512, 32 ** -0.5, None)
                        subnorm_pair(l, 6 + cp, c0, 512)
                if DBG == 7:
                    return
                outproj(l, xTs, SBK, 1)
                norm_mod(xTs, SBK, l, 1, 1)
                if DBG == 8:
                    return
                h3 = hst[:, :].rearrange("p (c t) -> p c t", t=2)
                S.op("pool", lambda e: e.tensor_copy(out=h3[:, :, 0], in_=hT[:, :, 1]), [hT[:, :, 1]], [hst[:, :]])
                S.op("pool", lambda e: e.tensor_copy(out=h3[:, :, 1], in_=hT[:, :, 1024]), [hT[:, :, 1024]], [hst[:, :]])
                hi = hin[l].ap(); ho = hout[l].ap()
                S.dma("sp", hi, hst[:, :])
                S.collective(lambda e: e.collective_compute("AllGather", ALU.bypass, replica_groups=RG, ins=[hi], outs=[ho]), [hi], [ho])
                S.dma("sp", candH[:, :, :], ho.rearrange("(r p) c -> p r c", p=128))
                for side in range(2):
                    dst = hT[:, :, 0] if side == 0 else hT[:, :, 1025]
                    for rr in range(4):
                        sc = sel[:, side * 4 + rr:side * 4 + rr + 1]
                        cv = candH[:, rr, :].rearrange("p (c t) -> p c t", t=2)[:, :, 1 - side]
                        if rr == 0:
                            S.op("dve", lambda e, dst=dst, cv=cv, sc=sc: e.tensor_scalar(out=dst, in0=cv, scalar1=sc, scalar2=None, op0=ALU.mult),
                                 [candH[:, rr, :], sc], [dst])
                        else:
                            S.op("dve", lambda e, dst=dst, cv=cv, sc=sc: e.scalar_tensor_tensor(out=dst, in0=cv, scalar=sc, in1=dst, op0=ALU.mult, op1=ALU.add),
                                 [candH[:, rr, :], sc, dst], [dst])
                if DBG == 9:
                    return
                ffn(l, xTs, 1, True)

            for pn in range(8):
                mod_panel(0, pn)
            mod_finish(0, 0, 2)
            mod_busy[0] = True
            for pn in range(8, 24):
                mod_tasks.append(lambda pn=pn: mod_panel(0, pn))

            def _fin0():
                mod_finish(0, 2, 6)
                mod_busy[0] = False
            mod_tasks.append(_fin0)
            load_x(xp, xTp, 4, 0)
            if run_s:
                load_x(xs, xTs, 8, 1)
            layer_P(0)
            mod_flush()
            if run_s:
                S1_hook[0] = True
                layer_S(0)
                mod_flush()
            else:
                for pn in range(24):
                    mod_panel(1, pn)
                mod_finish(1, 0, 6)
            layer_P(1)
            norm_mod(xTp, PB, 0, 0, 0, gain_only=pvec[:, PV_GF:PV_GF + 8])
            store_y(xTp, y_p, 4, 0)
            if run_s:
                layer_S(1)
                norm_mod(xTs, SBK, 0, 0, 1, gain_only=pvec[:, PV_GF:PV_GF + 8])
                store_y(xTs, y_s, 8, 1)

        program(DummySched())
        W.play()
        S = Sched(nc)
        program(S)
        S.emit(es)
    return nc


_NC_CACHE = {}


def _fm(v):
    return np.ascontiguousarray(np.asarray(v, np.float32).reshape(8, 128).T)


def _host_layout(inp):
    A_H, KVH, HD = 6, 2, 64
    w_in = np.asarray(inp["w_in"], np.float32)
    o_qa, o_ka, o_va = 0, 384, 512
    o_qb, o_kb, o_vb = 640, 1024, 1152
    o_qc, o_kc, o_vc = 1280, 1536, 1792

    def hcols(base, h):
        return list(range(base + h * 64, base + (h + 1) * 64))

    cols = []
    cols += list(range(o_ka, o_ka + 128))
    cols += list(range(o_kb, o_kb + 128))
    cols += list(range(o_kc, o_kc + 256))
    for j in range(3):
        cols += hcols(o_qa, j) + hcols(o_qa, 3 + j)
    for j in range(3):
        cols += hcols(o_qb, j) + hcols(o_qb, 3 + j)
    cols += list(range(o_qc, o_qc + 256))
    cols += list(range(o_va, o_va + 128)) + list(range(o_vb, o_vb + 128)) + list(range(o_vc, o_vc + 256))
    cols += list(range(o_ka, o_ka + 128)) + list(range(o_kb, o_kb + 128)) + list(range(o_kc, o_kc + 256))
    assert len(cols) == 2560
    w_inp = np.ascontiguousarray(w_in[:, :, cols])
    rows = []
    for base in (0, 384):
        for j in range(3):
            rows += hcols(base, j) + hcols(base, 3 + j)
    rows += list(range(768, 1024))
    w_outp = np.ascontiguousarray(np.asarray(inp["w_out"], np.float32)[:, rows, :])
    w_up = np.asarray(inp["w_up"], np.float32)
    w_upp = np.ascontiguousarray(
        np.stack([w_up[:, :, 0:DFF].reshape(DEPTH, D, NJ, 128), w_up[:, :, DFF:].reshape(DEPTH, D, NJ, 128)], axis=3)
        .reshape(DEPTH, D, NJ, 256))
    pvec = np.zeros((128, NPVEC), np.float32)
    for l in range(DEPTH):
        pvec[:, PV_GN + (l * 2) * 8:PV_GN + (l * 2) * 8 + 8] = _fm(inp["g_norm1"][l])
        pvec[:, PV_GN + (l * 2 + 1) * 8:PV_GN + (l * 2 + 1) * 8 + 8] = _fm(inp["g_norm2"][l])
        pvec[:, PV_BADA + l * 48:PV_BADA + (l + 1) * 48] = np.asarray(inp["b_ada"][l], np.float32).reshape(48, 128).T
        pvec[:, PV_GQK + l * 2] = np.tile(np.asarray(inp["g_qa"][l], np.float32), 2)
        pvec[:, PV_GQK + l * 2 + 1] = np.tile(np.asarray(inp["g_ka"][l], np.float32), 2)
        pvec[:, PV_GSUB + l] = np.tile(np.asarray(inp["g_subln"][l], np.float32), 2)
        cw = np.asarray(inp["conv_w"][l], np.float32)
        cb = np.asarray(inp["conv_b"][l], np.float32)
        for ag in range(2):
            for k in range(3):
                v = cw[k, ag * DFF:(ag + 1) * DFF].reshape(NJ, 128).T
                for j in range(NJ):
                    pvec[:, PV_CW + ((l * NJ + j) * 2 + ag) * 4 + k] = v[:, j]
            v = cb[ag * DFF:(ag + 1) * DFF].reshape(NJ, 128).T
            for j in range(NJ):
                pvec[:, PV_CW + ((l * NJ + j) * 2 + ag) * 4 + 3] = v[:, j]
        sk = np.asarray(inp["sink_b"][l], np.float32)
        for j in range(3):
            pvec[0:64, PV_SINK + l * 3 + j] = sk[j]
            pvec[64:128, PV_SINK + l * 3 + j] = sk[3 + j]
            pvec[0:64, PV_SINKX + l * 3 + j] = sk[3 + j]
            pvec[64:128, PV_SINKX + l * 3 + j] = sk[j]
    pvec[:, PV_GF:PV_GF + 8] = _fm(inp["g_final"])
    lamv = np.stack([np.stack([np.asarray(inp[n][l], np.float32) for n in ("lam_q1", "lam_k1", "lam_q2", "lam_k2")])
                     for l in range(DEPTH)]).reshape(1, -1)
    cm = np.zeros((128, 4, 128), np.float32)
    for m in range(128):
        d = m % 64
        partner = m + 16 if (d % 32) < 16 else m - 16
        cm[partner, 0, m] = 1.0
        d = m % 32
        partner = m + 8 if (d % 16) < 8 else m - 8
        cm[partner, 1, m] = 1.0
    for a in range(2):
        cm[a * 64:(a + 1) * 64, 2, a * 64:(a + 1) * 64] = 1.0
    cm[:, 3, :] = 1.0
    shared = {
        "w_ada": np.ascontiguousarray(np.asarray(inp["w_ada"], np.float32)),
        "w_inp": w_inp, "w_outp": w_outp, "w_upp": w_upp,
        "w_downp": np.ascontiguousarray(np.asarray(inp["w_down"], np.float32)),
        "pvec": pvec, "lamv": np.ascontiguousarray(lamv),
        "gka_row": np.ascontiguousarray(np.asarray(inp["g_ka"], np.float32)),
        "cmat": cm.astype(ml_dtypes.bfloat16),
    }
    jj = np.arange(128)[:, None]
    ii = np.arange(128)[None, :]
    triL = (jj >= ii).astype(np.float32)
    triR = (jj <= ii).astype(np.float32)
    per_core = []
    for c in range(8):
        b, r = c // 4, c % 4
        m = dict(shared)
        m["xp"] = np.ascontiguousarray(np.asarray(inp["x_prompt"], np.float32)[2 * c:2 * c + 2].reshape(512, D))
        m["xs"] = np.ascontiguousarray(np.asarray(inp["x_sample"], np.float32)[b, r * 1024:(r + 1) * 1024])
        m["cakT"] = np.ascontiguousarray(np.asarray(inp["cache_a_k"], np.float32)[b].reshape(DEPTH, 256, 128).transpose(0, 2, 1))
        m["cav"] = np.ascontiguousarray(np.asarray(inp["cache_a_v"], np.float32)[b].reshape(DEPTH, 256, 128))
        m["cbkT"] = np.ascontiguousarray(np.asarray(inp["cache_b_k"], np.float32)[b].reshape(DEPTH, 256, 128).transpose(0, 2, 1))
        m["cbv"] = np.ascontiguousarray(np.asarray(inp["cache_b_v"], np.float32)[b].reshape(DEPTH, 256, 128))
        m["cckT"] = np.ascontiguousarray(np.asarray(inp["cache_c_k"], np.float32)[b].reshape(DEPTH, 256, 256).transpose(0, 2, 1))
        m["ccv"] = np.ascontiguousarray(np.asarray(inp["cache_c_v"], np.float32)[b].reshape(DEPTH, 256, 256))
        cond2 = np.stack([np.asarray(inp["c_ctx"], np.float32), np.asarray(inp["c"], np.float32)[b]], axis=1)
        m["cond2T"] = np.ascontiguousarray(cond2.reshape(8, 128, 2).transpose(1, 0, 2))
        t = np.arange(r * 1024, (r + 1) * 1024)
        rows_, cols_ = (t // 64).astype(np.float64), (t % 64).astype(np.float64)
        rp = np.zeros((128, 4, 1024), np.float32)
        for p in range(128):
            d = p % 64
            pos = rows_ if d < 32 else cols_
            i = d % 16
            inv = np.float32(THETA) ** (-np.float32(i) / np.float32(16))
            ang = pos.astype(np.float32) * np.float32(inv)
            sgn = -1.0 if (d % 32) < 16 else 1.0
            rp[p, 0] = np.cos(ang); rp[p, 1] = sgn * np.sin(ang)
            d = p % 32
            pos = rows_ if d < 16 else cols_
            i = d % 8
            inv = np.float32(THETA) ** (-np.float32(i) / np.float32(8))
            ang = pos.astype(np.float32) * np.float32(inv)
            sgn = -1.0 if (d % 16) < 8 else 1.0
            rp[p, 2] = np.cos(ang); rp[p, 3] = sgn * np.sin(ang)
        m["rope"] = rp.astype(ml_dtypes.bfloat16)
        bm = np.stack([triL, triR, triL * (1.0 if r > 0 else 0.0), triR * (1.0 if r < 3 else 0.0)], axis=1)
        m["bmask"] = ((bm - 1.0) * 30000.0).astype(ml_dtypes.bfloat16)
        sl = np.zeros((128, 8), np.float32)
        if r > 0:
            sl[:, r - 1] = 1.0
        if r < 3:
            sl[:, 4 + r + 1] = 1.0
        m["sel"] = sl
        per_core.append(m)
    return per_core


RUN_S = True
DBG = 0


def kernel(**inputs):
    key = ("nc", RUN_S, DBG)
    if key not in _NC_CACHE:
        _NC_CACHE[key] = build_program(RUN_S)
    nc = _NC_CACHE[key]
    in_maps = _host_layout(inputs)
    res = run_bass_kernel_spmd(nc, in_maps, core_ids=list(range(8)))
    r = res.results
    y_prompt = np.concatenate([np.asarray(r[c]["y_p"], np.float32).reshape(2, 256, D) for c in range(8)], axis=0)
    if RUN_S:
        y_sample = np.stack([np.concatenate([np.asarray(r[b * 4 + q]["y_s"], np.float32) for q in range(4)], axis=0)
                             for b in range(2)], axis=0)
    else:
        y_sample = np.zeros((2, 4096, D), np.float32)

    def gat(name, kvh, hd):
        return np.concatenate([np.asarray(r[c][name], np.float32).reshape(2, DEPTH, 256, kvh, hd) for c in range(8)], axis=0)

    return (y_prompt, y_sample, gat("o_ak", 2, 64), gat("o_av", 2, 64), gat("o_bk", 2, 64), gat("o_bv", 2, 64),
            gat("o_ck", 4, 64), gat("o_cv", 4, 64))
```

```python
from contextlib import ExitStack
import math
import numpy as np
import ml_dtypes
import concourse.bass as bass
import concourse.mybir as mybir
from concourse.bass_utils import run_bass_kernel_spmd

F32 = mybir.dt.float32
BF16 = mybir.dt.bfloat16
AF = mybir.ActivationFunctionType
ALU = mybir.AluOpType
AX = mybir.AxisListType


class Sched:
    ENGS = ("pe", "act", "dve", "pool", "sp")
    NDS = {"sp": 24, "pool": 12}
    CAP = 30000

    def __init__(self, nc, same_eng_sync=True):
        self.nc = nc
        self.prog = {e: [] for e in self.ENGS}
        self.hist = {}
        self.ndma = {"sp": 0, "pool": 0}
        self.same_eng_sync = same_eng_sync

    @staticmethod
    def region(ap):
        t = ap.tensor
        shape = [int(s) for s in t.shape]
        off = int(ap.offset)
        dims = [(int(a), int(b)) for a, b in ap.ap]
        if isinstance(t, bass.DRamTensorHandle):
            ext = sum((c - 1) * abs(s) for s, c in dims) + 1
            return (t.name, 0, 1, off, off + ext)
        row = 1
        for s in shape[1:]:
            row *= s
        p0 = off // row
        f0 = off % row
        p1 = p0 + dims[0][1]
        ext = sum((c - 1) * abs(s) for s, c in dims[1:]) + 1
        return (t.name, p0, p1, f0, f0 + ext)

    def _track(self, who, reads, writes):
        deps = set()
        regs_r = [self.region(a) for a in reads]
        regs_w = [self.region(a) for a in writes]
        for (name, p0, p1, f0, f1) in regs_r:
            for rec in self.hist.get(name, ()):
                if rec[6] and rec[2] < p1 and p0 < rec[3] and rec[4] < f1 and f0 < rec[5]:
                    deps.add(rec[0])
        for (name, p0, p1, f0, f1) in regs_w:
            for rec in self.hist.get(name, ()):
                if rec[2] < p1 and p0 < rec[3] and rec[4] < f1 and f0 < rec[5]:
                    deps.add(rec[0])
        for (name, p0, p1, f0, f1) in regs_w:
            lst = self.hist.setdefault(name, [])
            lst[:] = [r for r in lst if not (p0 <= r[2] and r[3] <= p1 and f0 <= r[4] and r[5] <= f1)]
            lst.append((who, None, p0, p1, f0, f1, True))
        for (name, p0, p1, f0, f1) in regs_r:
            lst = self.hist.setdefault(name, [])
            lst[:] = [r for r in lst if not ((not r[6]) and r[0][0] == who[0] and r[0][0] != "dma"
                                             and r[2] == p0 and r[3] == p1 and r[4] == f0 and r[5] == f1)]
            lst.append((who, None, p0, p1, f0, f1, False))
        deps.discard(who)
        return deps

    def op(self, eng, fn, reads=(), writes=()):
        idx = len(self.prog[eng])
        who = (eng, idx)
        deps = self._track(who, reads, writes)
        if eng == "pe":
            deps = {d for d in deps if d[0] != "pe"}
        elif not self.same_eng_sync:
            deps = {d for d in deps if d[0] != eng}
        self.prog[eng].append({"kind": "op", "fn": fn, "deps": deps})
        return who

    def dma(self, q, out, in_, **kw):
        k = self.ndma[q]
        self.ndma[q] += 1
        who = ("dma", q, k)
        deps = self._track(who, [in_], [out])
        self.prog[q].append({"kind": "dma", "fn": (lambda e: e.dma_start(out=out, in_=in_, **kw)),
                             "deps": deps, "k": k})
        return who

    def collective(self, fn, reads, writes):
        k = self.ndma["pool"]
        self.ndma["pool"] += 1
        who = ("dma", "pool", k)
        deps = self._track(who, reads, writes)
        self.prog["pool"].append({"kind": "cc", "fn": fn, "deps": deps, "k": k})
        return who

    def emit(self, es):
        nc = self.nc
        needed = {e: set() for e in self.ENGS}
        for e in self.ENGS:
            for ins in self.prog[e]:
                for d in ins["deps"]:
                    if d[0] != "dma":
                        needed[d[0]].add(d[1])
        cnt = {}
        nsig = {}
        for e in self.ENGS:
            c = 0
            for i in range(len(self.prog[e])):
                if i in needed[e]:
                    c += 1
                    cnt[(e, i)] = c
            nsig[e] = c
        csem = {}
        for e in ("pe", "act", "dve", "pool"):
            n = max(1, (nsig[e] + self.CAP - 1) // self.CAP)
            csem[e] = [es.enter_context(nc.semaphore(f"c_{e}_{j}")) for j in range(n)]
        dsem = {q: [es.enter_context(nc.semaphore(f"d_{q}_{j}")) for j in range(self.NDS[q])]
                for q in ("sp", "pool")}
        ccs = {}
        for ins in self.prog["pool"]:
            if ins["kind"] == "cc":
                ccs[ins["k"]] = es.enter_context(nc.semaphore(f"cc_{ins['k']}"))
        ncc = len(ccs)
        dma_target = {}
        for q in ("sp", "pool"):
            per = [0] * self.NDS[q]
            nd = 0
            for ins in self.prog[q]:
                if ins["kind"] == "dma":
                    s = nd % self.NDS[q]
                    nd += 1
                    per[s] += 1
                    dma_target[(q, ins["k"])] = (dsem[q][s], 16 * per[s], 16 * (per[s] - 1))
                elif ins["kind"] == "cc":
                    dma_target[(q, ins["k"])] = (ccs[ins["k"]], 1, 0)

        def target(d):
            if d[0] == "dma":
                t = dma_target[(d[1], d[2])]
                return t[0], t[1]
            c = cnt[d] - 1
            return csem[d[0]][c // self.CAP], c % self.CAP + 1

        import os
        if os.environ.get("MK_STATS"):
            print("STATS nsig", nsig, "ndma", self.ndma, "ninstr", {e: len(self.prog[e]) for e in self.ENGS}, "ncc", ncc, flush=True)
        block = es.enter_context(nc.Block())

        def make(eng):
            def body(e):
                waited = {}
                for i, ins in enumerate(self.prog[eng]):
                    tg = {}
                    for d in ins["deps"]:
                        s, v = target(d)
                        key = id(s)
                        if key not in tg or tg[key][1] < v:
                            tg[key] = (s, v)
                    if ins["kind"] in ("dma", "cc"):
                        s, v, prev = dma_target[(eng, ins["k"])]
                        if prev > 0:
                            key = id(s)
                            if key not in tg or tg[key][1] < prev:
                                tg[key] = (s, prev)
                    for key, (s, v) in tg.items():
                        if waited.get(key, 0) < v:
                            e.wait_ge(s, v)
                            waited[key] = v
                    r = ins["fn"](e)
                    if ins["kind"] == "dma":
                        r.then_inc(dma_target[(eng, ins["k"])][0], 16)
                    elif ins["kind"] == "cc":
                        r.then_inc(dma_target[(eng, ins["k"])][0], 1)
                    elif (eng, i) in cnt:
                        c = cnt[(eng, i)] - 1
                        r.then_inc(csem[eng][c // self.CAP], 1)
                if eng in ("sp", "pool"):
                    last = {}
                    for ins in self.prog[eng]:
                        if ins["kind"] in ("dma", "cc"):
                            s, v, _ = dma_target[(eng, ins["k"])]
                            last[id(s)] = (s, v)
                    for key, (s, v) in last.items():
                        if waited.get(key, 0) < v:
                            e.wait_ge(s, v)
            return body

        block.tensor(make("pe"))
        block.scalar(make("act"))
        block.vector(make("dve"))
        block.gpsimd(make("pool"))
        block.sync(make("sp"))


class DummySched:
    def op(self, *a, **k):
        return None

    def dma(self, *a, **k):
        return None

    def collective(self, *a, **k):
        return None


class WStream:
    NSLOT = 6
    LOOK = 5

    def __init__(self, slots):
        self.slots = slots
        self.order = []
        self.mode = "record"
        self.i = 0
        self.issued = 0

    def play(self):
        self.mode = "play"
        self.i = 0
        self.issued = 0

    def use(self, S, src, kdim, n, look=None):
        if self.mode == "record":
            self.order.append((src, kdim, n))
            i = len(self.order) - 1
        else:
            i = self.i
            self.i += 1
            la = self.LOOK if look is None else look
            while self.issued < min(len(self.order), i + 1 + la):
                s2, k2, n2 = self.order[self.issued]
                S.dma("pool", self._view(self.issued, k2, n2), s2)
                self.issued += 1
        return self._view(i, kdim, n)

    def _view(self, i, kdim, n):
        sl = self.slots[i % self.NSLOT]
        return sl[:, 0:kdim * n].rearrange("p (k n) -> p k n", n=n)


D = 1024
DEPTH = 2
DFF = 2816
NJ = 22
EPS = 1e-6
THETA = 10000.0
LAM_INIT = [0.8 - 0.6 * math.exp(-0.3 * l) for l in range(DEPTH)]
CK_AK, CK_BK, CK_CK0, CK_CK1, CK_A0, CK_B0, CK_C0 = 0, 1, 2, 3, 4, 7, 10
NQK = 12
NPV = 8 * 4 + 8 + 96 + 4 + 2 + 2 * NJ * 2 * 4 + 6
PV_GN, PV_GF, PV_BADA, PV_GQK, PV_GSUB, PV_CW, PV_SINK = 0, 32, 40, 136, 140, 142, 142 + 2 * NJ * 2 * 4
PV_SINKX = PV_SINK + 6
NPVEC = PV_SINKX + 6


def build_program(run_s=True):
    nc = bass.Bass("TRN2", target_bir_lowering=False)

    def din(name, shape, dt=F32):
        return nc.dram_tensor(name, list(shape), dt, kind="ExternalInput").ap()

    def dout(name, shape, dt=F32):
        return nc.dram_tensor(name, list(shape), dt, kind="ExternalOutput").ap()

    xp = din("xp", [512, D])
    xs = din("xs", [1024, D])
    cak = din("cakT", [DEPTH, 128, 256]); cav = din("cav", [DEPTH, 256, 128])
    cbk = din("cbkT", [DEPTH, 128, 256]); cbv = din("cbv", [DEPTH, 256, 128])
    cck = din("cckT", [DEPTH, 256, 256]); ccv = din("ccv", [DEPTH, 256, 256])
    cond2T = din("cond2T", [128, 8, 2])
    w_ada = din("w_ada", [DEPTH, D, 6 * D])
    w_inp = din("w_inp", [DEPTH, D, 2560])
    w_outp = din("w_outp", [DEPTH, D, D])
    w_upp = din("w_upp", [DEPTH, D, NJ, 256])
    w_downp = din("w_downp", [DEPTH, DFF, D])
    pvec_d = din("pvec", [128, NPVEC])
    lamv_d = din("lamv", [1, DEPTH * 4 * 32])
    gka_d = din("gka_row", [DEPTH, 64])
    cmat_d = din("cmat", [128, 4, 128], BF16)
    rope_d = din("rope", [128, 4, 1024], BF16)
    bmask_d = din("bmask", [128, 4, 128], BF16)
    sel_d = din("sel", [128, 8])

    y_p = dout("y_p", [512, D])
    y_s = dout("y_s", [1024, D])
    o_ak = dout("o_ak", [2, DEPTH, 256, 128]); o_av = dout("o_av", [2, DEPTH, 256, 128])
    o_bk = dout("o_bk", [2, DEPTH, 256, 128]); o_bv = dout("o_bv", [2, DEPTH, 256, 128])
    o_ck = dout("o_ck", [2, DEPTH, 256, 256]); o_cv = dout("o_cv", [2, DEPTH, 256, 256])

    aginK = [nc.dram_tensor(f"aginK{l}", [512, 1024], BF16) for l in range(DEPTH)]
    agoutK = [nc.dram_tensor(f"agoutK{l}", [2048, 1024], BF16) for l in range(DEPTH)]
    aginV = [[nc.dram_tensor(f"aginV{l}_{i}", [384, 1024], BF16) for i in range(2)] for l in range(DEPTH)]
    agoutV = [[nc.dram_tensor(f"agoutV{l}_{i}", [1536, 1024], BF16) for i in range(2)] for l in range(DEPTH)]
    hin = [nc.dram_tensor(f"hin{l}", [128, 16], BF16) for l in range(DEPTH)]
    hout = [nc.dram_tensor(f"hout{l}", [512, 16], BF16) for l in range(DEPTH)]
    RG = [[0, 1, 2, 3], [4, 5, 6, 7]]

    with ExitStack() as es:
        def sb(name, shape, dt):
            return es.enter_context(nc.sbuf_tensor(name, list(shape), dt))

        xTp = sb("xTp", [128, 8, 512], F32)
        xTs = sb("xTs", [128, 8, 1026], F32)
        hT = sb("hT", [128, 8, 1026], BF16)
        qk = sb("qk", [128, NQK, 1026], BF16)
        rope = sb("ropet", [128, 4, 1024], BF16)
        gK = sb("gK", [128, 2, 4352], BF16)
        gV = sb("gV", [128, 34, 192], BF16)
        vloc = sb("vloc", [128, 3072], BF16)
        kBx = sb("kBx", [128, 4, 128], BF16)
        vB = sb("vB", [128, 13, 192], BF16)
        candH = sb("candH", [128, 4, 16], BF16)
        hst = sb("hst", [128, 16], BF16)
        PT = [sb(f"PT{i}", [128, 1024], BF16) for i in range(3)]
        wsl = [sb(f"wsl{i}", [128, 2048], BF16) for i in range(6)]
        tfp = sb("tfp", [128, 3, 1026], F32)
        tb = sb("tb", [128, 3, 1026], BF16)
        rst = sb("rst", [128, 2, 512], F32)
        ocT = sb("ocT", [128, 512], F32)
        cmat = sb("cmat_s", [128, 4, 128], BF16)
        bmask = sb("bmask_s", [128, 4, 128], BF16)
        identF = sb("identF", [128, 128], F32)
        identB = sb("identB", [128, 128], BF16)
        pvec = sb("pvec_s", [128, NPVEC], F32)
        modT = sb("modT", [128, DEPTH, 48, 2], F32)
        amod = sb("amod", [128, DEPTH, 2, 2, 8], F32)
        scT = sb("scT", [128, 8, 2], BF16)
        cstage = sb("cstage", [128, 8, 2], F32)
        lamw = sb("lamw", [128, DEPTH * 4 * 32], F32)
        lams = sb("lams", [128, 16], F32)
        gsub2 = sb("gsub2", [128, DEPTH], F32)
        esink = sb("esink", [128, 12], F32)
        gkar = sb("gkar", [128, DEPTH, 64], F32)
        sel = sb("sel_s", [128, 8], F32)
        epsc = sb("epsc", [128, 1], F32)
        sst = sb("sst", [128, 8], F32)
        qm = sb("qm", [128, 2, 512], BF16)
        ps = es.enter_context(nc.psum_tensor("ps", [128, 8, 512], F32))
        W = WStream(wsl)

        R64, R32, BD64, ONES = (cmat[:, i, :] for i in range(4))

        def program(S):
            def bank(b, w=512):
                return ps[:, b, 0:w]

            S.dma("sp", cmat[:, :, :], cmat_d)
            S.dma("sp", bmask[:, :, :], bmask_d)
            S.dma("sp", rope[:, :, :], rope_d)
            S.dma("sp", pvec[:, :], pvec_d)
            S.dma("sp", sel[:, :], sel_d)
            S.dma("sp", cstage[:, :, :], cond2T)
            S.dma("sp", lamw[:, :], lamv_d.partition_broadcast(128))
            for l in range(DEPTH):
                S.dma("sp", gkar[:, l, :], gka_d[l:l + 1, :].partition_broadcast(128))
            S.op("pool", lambda e: e.memset(identF[:, :], 1.0), [], [identF[:, :]])
            S.op("pool", lambda e: e.affine_select(out=identF[:, :], in_=identF[:, :], pattern=[[-1, 128]],
                                                    compare_op=ALU.is_equal, fill=0.0, base=0, channel_multiplier=1),
                 [identF[:, :]], [identF[:, :]])
            S.op("pool", lambda e: e.memset(epsc[:, :], EPS), [], [epsc[:, :]])
            S.op("pool", lambda e: e.tensor_copy(out=identB[:, :], in_=identF[:, :]), [identF[:, :]], [identB[:, :]])
            S.op("pool", lambda e: e.memset(gV[:, :, 64:128], 1.0), [], [gV[:, :, 64:128]])
            S.op("pool", lambda e: e.memset(vB[:, :, 64:128], 1.0), [], [vB[:, :, 64:128]])
            S.op("pool", lambda e: e.memset(qm[:, :, :], 0.0), [], [qm[:, :, :]])
            S.op("act", lambda e: e.activation(out=scT[:, :, :], in_=cstage[:, :, :], func=AF.Silu),
                 [cstage[:, :, :]], [scT[:, :, :]])
            S.op("act", lambda e: e.activation(out=esink[:, :], in_=pvec[:, PV_SINK:PV_SINK + 12], func=AF.Exp),
                 [pvec[:, PV_SINK:PV_SINK + 12]], [esink[:, :]])
            for l in range(DEPTH):
                for t in range(2):
                    a0 = lamw[:, (l * 4 + 2 * t) * 32:(l * 4 + 2 * t + 1) * 32]
                    a1 = lamw[:, (l * 4 + 2 * t + 1) * 32:(l * 4 + 2 * t + 2) * 32]
                    tmp = tfp[:, 0, 0:32]
                    S.op("dve", lambda e, a0=a0, a1=a1, tmp=tmp: e.tensor_tensor(out=tmp, in0=a0, in1=a1, op=ALU.mult),
                         [a0, a1], [tmp])
                    dst = lams[:, l * 4 + 2 + t:l * 4 + 3 + t]
                    S.op("dve", lambda e, tmp=tmp, dst=dst: e.reduce_sum(out=dst, in_=tmp, axis=AX.X), [tmp], [dst])
                    S.op("act", lambda e, dst=dst: e.activation(out=dst, in_=dst, func=AF.Exp), [dst], [dst])
                e1 = lams[:, l * 4 + 2:l * 4 + 3]; e2 = lams[:, l * 4 + 3:l * 4 + 4]
                lam = lams[:, l * 4:l * 4 + 1]; nlam = lams[:, l * 4 + 1:l * 4 + 2]
                S.op("dve", lambda e, e1=e1, e2=e2, lam=lam, l=l: e.scalar_tensor_tensor(
                    out=lam, in0=e1, scalar=LAM_INIT[l], in1=e2, op0=ALU.add, op1=ALU.subtract), [e1, e2], [lam])
                S.op("dve", lambda e, lam=lam, nlam=nlam: e.tensor_scalar(out=nlam, in0=lam, scalar1=-1.0, scalar2=None, op0=ALU.mult),
                     [lam], [nlam])
                gs = gsub2[:, l:l + 1]; src = pvec[:, PV_GSUB + l:PV_GSUB + l + 1]
                S.op("dve", lambda e, gs=gs, src=src, l=l: e.tensor_scalar(out=gs, in0=src, scalar1=1.0 - LAM_INIT[l], scalar2=None, op0=ALU.mult),
                     [src], [gs])

            mod_tasks = []
            mod_busy = [False]
            S1_hook = [False]

            def mod_panel(l, pn):
                src = w_ada[l, :, pn * 256:(pn + 1) * 256].rearrange("(k p) n -> p k n", p=128)
                wv = W.use(S, src, 8, 256)
                for h in range(2):
                    idx = pn * 2 + h
                    o = ps[:, 7, idx * 2:idx * 2 + 2]
                    for k in range(8):
                        S.op("pe", lambda e, o=o, wv=wv, k=k, h=h: e.matmul(
                            o, lhsT=wv[:, k, h * 128:(h + 1) * 128], rhs=scT[:, k, :], start=(k == 0), stop=(k == 7)),
                            [wv[:, k, h * 128:(h + 1) * 128], scT[:, k, :]], [o])

            def mod_finish(l, v0, v1):
                src = ps[:, 7, 0:96].rearrange("p (i c) -> p i c", c=2)
                for cnd in range(2):
                    o = modT[:, l, v0 * 8:v1 * 8, cnd]; i0 = src[:, v0 * 8:v1 * 8, cnd]
                    i1 = pvec[:, PV_BADA + l * 48 + v0 * 8:PV_BADA + l * 48 + v1 * 8]
                    S.op("dve", lambda e, o=o, i0=i0, i1=i1: e.tensor_tensor(out=o, in0=i0, in1=i1, op=ALU.add), [i0, i1], [o])
                for n in range(2):
                    if not (v0 <= 1 + 3 * n < v1):
                        continue
                    for cnd in range(2):
                        o = amod[:, l, n, cnd, :]
                        sc = modT[:, l, (1 + 3 * n) * 8:(1 + 3 * n) * 8 + 8, cnd]
                        g = pvec[:, PV_GN + (l * 2 + n) * 8:PV_GN + (l * 2 + n) * 8 + 8]
                        S.op("dve", lambda e, o=o, sc=sc, g=g: e.scalar_tensor_tensor(
                            out=o, in0=sc, scalar=1.0, in1=g, op0=ALU.add, op1=ALU.mult), [sc, g], [o])

            def mod_pump(n):
                for _ in range(n):
                    if mod_tasks:
                        mod_tasks.pop(0)()

            def mod_flush():
                while mod_tasks:
                    mod_tasks.pop(0)()

            def mod(l, v, cnd, c):
                return modT[:, l, v * 8 + c, cnd:cnd + 1]

            def rstd_from_sum(pssum, w, scale, slot):
                o = rst[:, slot, 0:w]
                S.op("act", lambda e: e.activation(out=o, in_=pssum, func=AF.Sqrt, scale=scale, bias=epsc[:, 0:1]),
                     [pssum, epsc[:, 0:1]], [o])
                S.op("dve", lambda e: e.reciprocal(out=o, in_=o), [o], [o])
                return o

            def norm_mod(xT, blocks, l, n, cnd, gain_only=None):
                for (c0, c1) in blocks:
                    w = c1 - c0
                    acc = bank(6, w)
                    for c in range(8):
                        sq = tb[:, c % 2, 0:w]
                        xin = xT[:, c, c0:c1]
                        S.op("act", lambda e, sq=sq, xin=xin: e.activation(out=sq, in_=xin, func=AF.Square), [xin], [sq])
                        S.op("pe", lambda e, acc=acc, sq=sq, c=c: e.matmul(acc, lhsT=ONES, rhs=sq, start=(c == 0), stop=(c == 7)),
                             [ONES, sq], [acc])
                    r = rstd_from_sum(acc, w, 1.0 / D, 0)
                    for c in range(8):
                        xin = xT[:, c, c0:c1]
                        if gain_only is None:
                            t = tfp[:, c % 2, 0:w]
                            S.op("dve", lambda e, t=t, xin=xin, r=r: e.tensor_tensor(out=t, in0=xin, in1=r, op=ALU.mult), [xin, r], [t])
                            o = hT[:, c, c0:c1]
                            a = amod[:, l, n, cnd, c:c + 1]; sh = mod(l, 3 * n, cnd, c)
                            S.op("act", lambda e, o=o, t=t, a=a, sh=sh: e.activation(out=o, in_=t, func=AF.Identity, scale=a, bias=sh),
                                 [t, a, sh], [o])
                        else:
                            g = gain_only[:, c:c + 1]
                            S.op("dve", lambda e, xin=xin, r=r, g=g: e.scalar_tensor_tensor(
                                out=xin, in0=xin, scalar=g, in1=r, op0=ALU.mult, op1=ALU.mult), [xin, r, g], [xin])

            def load_x(xd, xT, ntile, col0):
                for g0 in range(0, ntile, 2):
                    for tt in range(2):
                        S.dma("sp", tfp[:, tt, 0:1024], xd[(g0 + tt) * 128:(g0 + tt + 1) * 128, :])
                    for c in range(8):
                        b = ps[:, c % 4, 0:256]
                        for tt in range(2):
                            src = tfp[:, tt, c * 128:(c + 1) * 128]
                            o = ps[:, c % 4, tt * 128:(tt + 1) * 128]
                            S.op("pe", lambda e, o=o, src=src: e.transpose(out=o, in_=src, identity=identF[:, :]),
                                 [src, identF[:, :]], [o])
                        dst = xT[:, c, col0 + g0 * 128:col0 + (g0 + 2) * 128]
                        S.op("dve", lambda e, dst=dst, b=b: e.tensor_copy(out=dst, in_=b), [b], [dst])

            def store_y(xT, yd, ntile, col0):
                for tt in range(ntile):
                    st = tfp[:, tt % 3, 0:1024]
                    for half in range(2):
                        for cc in range(4):
                            c = half * 4 + cc
                            src = xT[:, c, col0 + tt * 128:col0 + (tt + 1) * 128]
                            o = ps[:, half, cc * 128:(cc + 1) * 128]
                            S.op("pe", lambda e, o=o, src=src: e.transpose(out=o, in_=src, identity=identF[:, :]),
                                 [src, identF[:, :]], [o])
                        b = ps[:, half, :]
                        dst = st[:, half * 512:(half + 1) * 512]
                        S.op("dve" if half == 0 else "act",
                             (lambda e, dst=dst, b=b: e.tensor_copy(out=dst, in_=b)) if half == 0 else
                             (lambda e, dst=dst, b=b: e.copy(out=dst, in_=b)), [b], [dst])
                    S.dma("sp", yd[tt * 128:(tt + 1) * 128, :], st)

            def finalize_pair(l, chunk, c0, w, kind, j=0, ob=4):
                f0 = (ob // 2 % 2) * 512
                for half in range(2):
                    lo = half * 64
                    dp = 64 - lo
                    den = ps[dp:dp + 64, ob + half, 0:w]
                    rd = tfp[lo:lo + 64, 2, f0:f0 + w]
                    if kind == "B":
                        sk = esink[dp:dp + 64, 6 + l * 3 + j:6 + l * 3 + j + 1]
                        S.op("dve", lambda e, rd=rd, den=den, sk=sk: e.tensor_scalar(out=rd, in0=den, scalar1=sk, scalar2=None, op0=ALU.add), [den, sk], [rd])
                        S.op("act", lambda e, rd=rd: e.activation(out=rd, in_=rd, func=AF.Ln), [rd], [rd])
                    else:
                        S.op("act", lambda e, rd=rd, den=den: e.activation(out=rd, in_=den, func=AF.Ln), [den], [rd])
                rall = tfp[:, 2, f0:f0 + w]
                S.op("act", lambda e: e.activation(out=rall, in_=rall, func=AF.Exp, scale=-1.0), [rall], [rall])
                for half in range(2):
                    lo = half * 64
                    O = ps[lo:lo + 64, ob + half, 0:w]
                    rd = tfp[lo:lo + 64, 2, f0:f0 + w]
                    o = hT[lo:lo + 64, chunk, c0:c0 + w]
                    S.op("dve", lambda e, o=o, O=O, rd=rd: e.tensor_tensor(out=o, in0=O, in1=rd, op=ALU.mult), [O, rd], [o])

            obank = [4]

            def dense_pair(l, qchunk, kfn, vfn, nkt, qc0, w, scale, ochunk, oc0, kind, j=0):
                if mod_busy[0]:
                    ob = 4
                else:
                    ob = obank[0]
                    obank[0] = 10 - ob

                def emit_s(kt):
                    slot = kt % 2
                    for half in range(2):
                        lo = half * 64
                        o = ps[:, slot * 2 + half, 0:w]
                        kk = kfn(kt)[lo:lo + 64, :]
                        qq = qk[lo:lo + 64, qchunk, qc0:qc0 + w]
                        S.op("pe", lambda e, o=o, kk=kk, qq=qq: e.matmul(o, lhsT=kk, rhs=qq, start=True, stop=True), [kk, qq], [o])

                def emit_exp(kt):
                    slot = kt % 2
                    pt = PT[kt % 3]
                    src = ps[:, slot * 2:slot * 2 + 2, 0:w]
                    dst = pt[:, :].rearrange("p (a b) -> p a b", a=2)[:, :, 0:w]
                    S.op("act", lambda e, dst=dst, src=src: e.activation(out=dst, in_=src, func=AF.Exp, scale=scale), [src], [dst])

                def emit_pv(kt):
                    pt = PT[kt % 3]
                    for half in range(2):
                        o = ps[:, ob + half, 0:w]
                        vv = vfn(kt)[:, half * 64:half * 64 + 128]
                        pp = pt[:, half * 512:half * 512 + w]
                        S.op("pe", lambda e, o=o, vv=vv, pp=pp, kt=kt: e.matmul(o, lhsT=vv, rhs=pp, start=(kt == 0), stop=(kt == nkt - 1)),
                             [vv, pp], [o])

                emit_s(0)
                for kt in range(nkt):
                    if kt + 1 < nkt:
                        emit_s(kt + 1)
                    emit_exp(kt)
                    emit_pv(kt)
                finalize_pair(l, ochunk, oc0, w, kind, j, ob)
                mod_pump(1)

            def diff_head(l, h, qfn, kfn, vfn, nkt, w, scale, oc_rows):
                hh = h % 2
                lo = hh * 64

                def emit_s(kt):
                    slot = kt % 2
                    for m in range(2):
                        o = ps[:, slot * 2 + m, 0:w]
                        if hh == 1 and m == 1:
                            kk = kfn(kt)[64:128, :]; qq = qfn(True)[64:128, :]
                        else:
                            r0 = lo + 32 * m
                            kk = kfn(kt)[r0:r0 + 32, :]; qq = qfn(False)[r0:r0 + 32, :]
                        S.op("pe", lambda e, o=o, kk=kk, qq=qq: e.matmul(o, lhsT=kk, rhs=qq, start=True, stop=True), [kk, qq], [o])

                def emit_exp(kt):
                    slot = kt % 2
                    pt = PT[kt % 3]
                    src = ps[:, slot * 2:slot * 2 + 2, 0:w]
                    dst = pt[:, :].rearrange("p (a b) -> p a b", a=2)[:, :, 0:w]
                    S.op("act", lambda e, dst=dst, src=src: e.activation(out=dst, in_=src, func=AF.Exp, scale=scale), [src], [dst])

                def emit_pv(kt):
                    pt = PT[kt % 3]
                    for m in range(2):
                        o = ps[:, 4 + m, 0:w]
                        vv = vfn(kt)[:, hh * 64:hh * 64 + 128]
                        pp = pt[:, m * 512:m * 512 + w]
                        S.op("pe", lambda e, o=o, vv=vv, pp=pp, kt=kt: e.matmul(o, lhsT=vv, rhs=pp, start=(kt == 0), stop=(kt == nkt - 1)),
                             [vv, pp], [o])

                emit_s(0)
                for kt in range(nkt):
                    if kt + 1 < nkt:
                        emit_s(kt + 1)
                    emit_exp(kt)
                    emit_pv(kt)
                dp = 64 - lo
                res = []
                for m in range(2):
                    O = ps[lo:lo + 64, 4 + m, 0:w]; den = ps[dp:dp + 64, 4 + m, 0:w]
                    rd = tfp[lo:lo + 64, 2, m * 512:m * 512 + w]
                    S.op("act", lambda e, rd=rd, den=den: e.activation(out=rd, in_=den, func=AF.Ln), [den], [rd])
                    if w != 512:
                        S.op("act", lambda e, rd=rd: e.activation(out=rd, in_=rd, func=AF.Exp, scale=-1.0), [rd], [rd])
                    elif m == 1:
                        rboth = tfp[lo:lo + 64, 2, 0:1024]
                        S.op("act", lambda e, rboth=rboth: e.activation(out=rboth, in_=rboth, func=AF.Exp, scale=-1.0), [rboth], [rboth])
                    res.append((O, rd))
                for (O, rd) in res:
                    S.op("dve", lambda e, rd=rd, O=O: e.tensor_tensor(out=rd, in0=O, in1=rd, op=ALU.mult), [O, rd], [rd])
                res = [r[1] for r in res]
                if False:
                    pass
                mod_pump(3)
                o = ocT[lo:lo + 64, 0:w]
                nl = lams[lo:lo + 64, l * 4 + 1:l * 4 + 2]
                S.op("dve", lambda e, o=o, a=res[0], b=res[1], nl=nl: e.scalar_tensor_tensor(
                    out=o, in0=b, scalar=nl, in1=a, op0=ALU.mult, op1=ALU.add), [res[0], res[1], nl], [o])

            def subnorm_pair(l, ochunk, oc0, w):
                sq = tb[:, 2, 0:w]
                S.op("act", lambda e: e.activation(out=sq, in_=ocT[:, 0:w], func=AF.Square), [ocT[:, 0:w]], [sq])
                acc = bank(6, w)
                S.op("pe", lambda e: e.matmul(acc, lhsT=BD64, rhs=sq, start=True, stop=True), [BD64, sq], [acc])
                r = rstd_from_sum(acc, w, 1.0 / 64, 1)
                o = hT[:, ochunk, oc0:oc0 + w]
                gs = gsub2[:, l:l + 1]
                S.op("dve", lambda e: e.scalar_tensor_tensor(out=o, in0=ocT[:, 0:w], scalar=gs, in1=r, op0=ALU.mult, op1=ALU.mult),
                     [ocT[:, 0:w], gs, r], [o])

            def qk_post(l, cc, z, c0, w, rope_on):
                o = qk[:, cc, c0:c0 + w]
                isA = cc in (CK_AK, CK_A0, CK_A0 + 1, CK_A0 + 2)
                isC = cc in (CK_CK0, CK_CK1, CK_C0, CK_C0 + 1)
                if isA:
                    g = pvec[:, PV_GQK + l * 2 + (1 if cc == CK_AK else 0):PV_GQK + l * 2 + (1 if cc == CK_AK else 0) + 1]
                    sq = tb[:, 2, 0:w]
                    S.op("act", lambda e: e.activation(out=sq, in_=z, func=AF.Square), [z], [sq])
                    acc = bank(6, w)
                    S.op("pe", lambda e: e.matmul(acc, lhsT=BD64, rhs=sq, start=True, stop=True), [BD64, sq], [acc])
                    r = rstd_from_sum(acc, w, 1.0 / 64, 1)
                    if not rope_on:
                        S.op("dve", lambda e: e.scalar_tensor_tensor(out=o, in0=z, scalar=g, in1=r, op0=ALU.mult, op1=ALU.mult), [z, g, r], [o])
                        return
                    zg = tb[:, 0, 0:w]
                    S.op("dve", lambda e: e.tensor_scalar(out=zg, in0=z, scalar1=g, scalar2=None, op0=ALU.mult), [z, g], [zg])
                else:
                    if not rope_on:
                        S.op("act", lambda e: e.copy(out=o, in_=z), [z], [o])
                        return
                    zg = tb[:, 0, 0:w]
                    S.op("act", lambda e: e.copy(out=zg, in_=z), [z], [zg])
                    r = None
                Rm = R32 if isC else R64
                ti = 2 if isC else 0
                cs = rope[:, ti, c0 - 1:c0 - 1 + w]; sn = rope[:, ti + 1, c0 - 1:c0 - 1 + w]
                rot = bank(7, w)
                S.op("pe", lambda e: e.matmul(rot, lhsT=Rm, rhs=zg, start=True, stop=True), [Rm, zg], [rot])
                t1 = tfp[:, 0, 0:w]; t2 = tfp[:, 1, 0:w]
                S.op("dve", lambda e: e.tensor_tensor(out=t1, in0=zg, in1=cs, op=ALU.mult), [zg, cs], [t1])
                S.op("dve", lambda e: e.tensor_tensor(out=t2, in0=rot, in1=sn, op=ALU.mult), [rot, sn], [t2])
                if r is None:
                    S.op("dve", lambda e: e.tensor_tensor(out=o, in0=t1, in1=t2, op=ALU.add), [t1, t2], [o])
                else:
                    S.op("dve", lambda e: e.tensor_tensor(out=t1, in0=t1, in1=t2, op=ALU.add), [t1, t2], [t1])
                    S.op("dve", lambda e: e.tensor_tensor(out=o, in0=t1, in1=r, op=ALU.mult), [t1, r], [o])

            def inproj_qk(l, panel, blocks, rope_on, bb):
                src = w_inp[l, :, panel * 256:(panel + 1) * 256].rearrange("(k p) n -> p k n", p=128)
                wv = W.use(S, src, 8, 256)
                for h in range(2):
                    cc = panel * 2 + h
                    for (c0, c1) in blocks:
                        w = c1 - c0
                        z = bank(bb[0] % 4, w)
                        bb[0] += 1
                        for k in range(8):
                            lh = wv[:, k, h * 128:(h + 1) * 128]; rh = hT[:, k, c0:c1]
                            S.op("pe", lambda e, z=z, lh=lh, rh=rh, k=k: e.matmul(z, lhsT=lh, rhs=rh, start=(k == 0), stop=(k == 7)), [lh, rh], [z])
                        qk_post(l, cc, z, c0, w, rope_on)

            def outproj(l, xT, blocks, cnd):
                mod_flush()
                bbi = 0
                for pn in range(4):
                    src = w_outp[l, :, pn * 256:(pn + 1) * 256].rearrange("(k p) n -> p k n", p=128)
                    wv = W.use(S, src, 8, 256)
                    for h in range(2):
                        m = pn * 2 + h
                        for (c0, c1) in blocks:
                            w = c1 - c0
                            y = bank(bbi % 4, w); bbi += 1
                            for k in range(8):
                                lh = wv[:, k, h * 128:(h + 1) * 128]; rh = hT[:, k, c0:c1]
                                S.op("pe", lambda e, y=y, lh=lh, rh=rh, k=k: e.matmul(y, lhsT=lh, rhs=rh, start=(k == 0), stop=(k == 7)), [lh, rh], [y])
                            xx = xT[:, m, c0:c1]; gt = mod(l, 2, cnd, m)
                            S.op("dve", lambda e, xx=xx, y=y, gt=gt: e.scalar_tensor_tensor(out=xx, in0=y, scalar=gt, in1=xx, op0=ALU.mult, op1=ALU.add),
                                 [y, gt, xx], [xx])

            def ffn(l, xT, cnd, is_s):
                if is_s:
                    ublocks = [(0, 512), (512, 1024), (1024, 1026)]
                    ncol = 1026
                    dblocks = [(1, 513), (513, 1025)]
                else:
                    ublocks = [(0, 512)]
                    ncol = 512
                    dblocks = [(0, 512)]
                for g in range(2):
                    for jj in range(11):
                        j = g * 11 + jj
                        src = w_upp[l, :, j, :].rearrange("(k p) n -> p k n", p=128)
                        wv = W.use(S, src, 8, 256)
                        cres = []
                        for ag in range(2):
                            b0 = ag * 3
                            u = ps[:, b0:b0 + 3, :].rearrange("p a b -> p (a b)")
                            for (c0, c1) in ublocks:
                                for k in range(8):
                                    lh = wv[:, k, ag * 128:(ag + 1) * 128]; rh = hT[:, k, c0:c1]; o = u[:, c0:c1]
                                    S.op("pe", lambda e, o=o, lh=lh, rh=rh, k=k: e.matmul(o, lhsT=lh, rhs=rh, start=(k == 0), stop=(k == 7)), [lh, rh], [o])
                            base = PV_CW + ((l * NJ + j) * 2 + ag) * 4
                            w0 = pvec[:, base:base + 1]; w1 = pvec[:, base + 1:base + 2]; w2 = pvec[:, base + 2:base + 3]; cb = pvec[:, base + 3:base + 4]
                            c = tfp[:, ag, 0:ncol]
                            S.op("act", lambda e, c=c, u=u, w1=w1, cb=cb: e.activation(out=c, in_=u[:, 0:ncol], func=AF.Identity, scale=w1, bias=cb),
                                 [u[:, 0:ncol], w1, cb], [c])
                            if is_s:
                                a1 = c[:, 1:1025]
                                S.op("dve", lambda e, a1=a1, u=u, w0=w0: e.scalar_tensor_tensor(out=a1, in0=u[:, 0:1024], scalar=w0, in1=a1, op0=ALU.mult, op1=ALU.add),
                                     [u[:, 0:1024], w0, a1], [a1])
                                S.op("dve", lambda e, a1=a1, u=u, w2=w2: e.scalar_tensor_tensor(out=a1, in0=u[:, 2:1026], scalar=w2, in1=a1, op0=ALU.mult, op1=ALU.add),
                                     [u[:, 2:1026], w2, a1], [a1])
                            else:
                                c3 = c.rearrange("p (b t) -> p b t", b=2); u3 = u[:, 0:512].rearrange("p (b t) -> p b t", b=2)
                                S.op("dve", lambda e, c3=c3, u3=u3, w0=w0: e.scalar_tensor_tensor(out=c3[:, :, 1:256], in0=u3[:, :, 0:255], scalar=w0, in1=c3[:, :, 1:256], op0=ALU.mult, op1=ALU.add),
                                     [u[:, 0:512], w0, c], [c])
                                S.op("dve", lambda e, c3=c3, u3=u3, w2=w2: e.scalar_tensor_tensor(out=c3[:, :, 0:255], in0=u3[:, :, 1:256], scalar=w2, in1=c3[:, :, 0:255], op0=ALU.mult, op1=ALU.add),
                                     [u[:, 0:512], w2, c], [c])
                            cres.append(c)
                        sa = tb[:, 0, 0:ncol]
                        S.op("act", lambda e, sa=sa, c=cres[0]: e.activation(out=sa, in_=c, func=AF.Silu), [cres[0]], [sa])
                        dst = qk[:, jj, 0:ncol]
                        S.op("pool", lambda e, dst=dst, sa=sa, cg=cres[1]: e.tensor_tensor(out=dst, in0=sa, in1=cg, op=ALU.mult), [sa, cres[1]], [dst])
                    bbi = 0
                    for m in range(8):
                        src = w_downp[l, g * 11 * 128:(g + 1) * 11 * 128, m * 128:(m + 1) * 128].rearrange("(k p) n -> p k n", p=128)
                        wv = W.use(S, src, 11, 128)
                        for (c0, c1) in dblocks:
                            w = c1 - c0
                            y = bank(6 + bbi % 2, w); bbi += 1
                            for k in range(11):
                                lh = wv[:, k, :]; rh = qk[:, k, c0:c1]
                                S.op("pe", lambda e, y=y, lh=lh, rh=rh, k=k: e.matmul(y, lhsT=lh, rhs=rh, start=(k == 0), stop=(k == 10)), [lh, rh], [y])
                            xx = xT[:, m, c0:c1]; gt = mod(l, 5, cnd, m)
                            S.op("dve", lambda e, xx=xx, y=y, gt=gt: e.scalar_tensor_tensor(out=xx, in0=y, scalar=gt, in1=xx, op0=ALU.mult, op1=ALU.add),
                                 [y, gt, xx], [xx])

            PB = [(0, 512)]

            def layer_P(l):
                norm_mod(xTp, PB, l, 0, 0)
                bb = [0]
                for panel in range(6):
                    inproj_qk(l, panel, PB, False, bb)
                vP = vloc[:, 0:3072].rearrange("p (t f) -> p t f", f=768)
                S.op("pool", lambda e: e.memset(vP[:, :, 64:128], 1.0), [], [vP[:, :, 64:128]])
                S.op("pool", lambda e: e.memset(vP[:, :, 256:320], 1.0), [], [vP[:, :, 256:320]])
                S.op("pool", lambda e: e.memset(vP[:, :, 448:512], 1.0), [], [vP[:, :, 448:512]])
                S.op("pool", lambda e: e.memset(vP[:, :, 640:704], 1.0), [], [vP[:, :, 640:704]])
                wvs = []
                for pn in range(6, 10):
                    src = w_inp[l, :, pn * 256:(pn + 1) * 256].rearrange("(k p) n -> p k n", p=128)
                    wvs.append(W.use(S, src, 8, 256, look=9 - pn))
                for tt in range(4):
                    kv = tfp[:, tt % 3, 0:1024]
                    for pi in range(4):
                        z = ps[:, pi, 0:256]
                        for k in range(8):
                            lh = hT[:, k, tt * 128:(tt + 1) * 128]; rh = wvs[pi][:, k, :]
                            S.op("pe", lambda e, z=z, lh=lh, rh=rh, k=k: e.matmul(z, lhsT=lh, rhs=rh, start=(k == 0), stop=(k == 7)), [lh, rh], [z])
                        dst = kv[:, pi * 256:(pi + 1) * 256]
                        if pi % 2 == 0:
                            S.op("dve", lambda e, dst=dst, z=z: e.tensor_copy(out=dst, in_=z), [z], [dst])
                        else:
                            S.op("act", lambda e, dst=dst, z=z: e.copy(out=dst, in_=z), [z], [dst])
                    for g in range(2):
                        kag = kv[:, 512 + g * 64:512 + (g + 1) * 64]
                        sq = tb[:, 0, g * 64:(g + 1) * 64]
                        S.op("dve", lambda e, sq=sq, kag=kag: e.tensor_tensor(out=sq, in0=kag, in1=kag, op=ALU.mult), [kag], [sq])
                        ss = sst[:, g:g + 1]
                        S.op("dve", lambda e, ss=ss, sq=sq: e.reduce_sum(out=ss, in_=sq, axis=AX.X), [sq], [ss])
                        S.op("act", lambda e, ss=ss: e.activation(out=ss, in_=ss, func=AF.Sqrt, scale=1.0 / 64, bias=epsc[:, 0:1]), [ss, epsc[:, 0:1]], [ss])
                        S.op("dve", lambda e, ss=ss: e.reciprocal(out=ss, in_=ss), [ss], [ss])
                        S.op("dve", lambda e, kag=kag, ss=ss, l=l: e.scalar_tensor_tensor(out=kag, in0=kag, scalar=ss, in1=gkar[:, l, :], op0=ALU.mult, op1=ALU.mult),
                             [kag, ss, gkar[:, l, :]], [kag])
                    vt = vP[:, tt, :]
                    for mx in range(2):
                        o = vt[:, mx * 192:mx * 192 + 256].rearrange("p (g x) -> p g x", x=128)[:, :, 0:64]
                        i = kv[:, mx * 128:(mx + 1) * 128].rearrange("p (g x) -> p g x", x=64)
                        S.op("pool", lambda e, o=o, i=i: e.tensor_copy(out=o, in_=i), [kv[:, mx * 128:(mx + 1) * 128]], [vt[:, mx * 192:mx * 192 + 192]])
                    vc3 = vt[:, 384:768].rearrange("p (a r) -> p a r", a=2)
                    i4 = kv[:, 256:512].rearrange("p (a g x) -> p a g x", a=2, g=2)
                    for g in range(2):
                        o = vc3[:, :, g * 128:g * 128 + 64]; i = i4[:, :, g, :]
                        S.op("pool", lambda e, o=o, i=i: e.tensor_copy(out=o, in_=i), [kv[:, 256:512]], [vt[:, 384:768]])
                    bi, t0 = tt // 2, (tt % 2) * 128
                    S.dma("sp", o_av[bi, l, t0:t0 + 128, :], kv[:, 0:128])
                    S.dma("sp", o_bv[bi, l, t0:t0 + 128, :], kv[:, 128:256])
                    S.dma("sp", o_cv[bi, l, t0:t0 + 128, :], kv[:, 256:512])
                    S.dma("sp", o_ak[bi, l, t0:t0 + 128, :], kv[:, 512:640])
                    S.dma("sp", o_bk[bi, l, t0:t0 + 128, :], kv[:, 640:768])
                    S.dma("sp", o_ck[bi, l, t0:t0 + 128, :], kv[:, 768:1024])
                for bi in range(2):
                    q0 = bi * 256
                    for j in range(3):
                        dense_pair(l, CK_A0 + j, lambda kt: qk[:, CK_AK, q0 + kt * 128:q0 + (kt + 1) * 128],
                                   lambda kt: vP[:, bi * 2 + kt, 0:192], 2, q0, 256, 0.125, j, q0, "A")
                    for j in range(3):
                        dense_pair(l, CK_B0 + j, lambda kt: qk[:, CK_BK, q0 + kt * 128:q0 + (kt + 1) * 128],
                                   lambda kt: vP[:, bi * 2 + kt, 192:384], 2, q0, 256, 0.125, 3 + j, q0, "B", j)
                    for cp in range(2):
                        src = qk[64:128, CK_C0 + cp, q0:q0 + 256]
                        S.op("pool", lambda e, src=src, cp=cp: e.tensor_copy(out=qm[64:128, cp, 0:256], in_=src), [src], [qm[64:128, cp, 0:256]])
                        S.op("pool", lambda e, cp=cp: e.memset(qm[64:96, cp, 0:256], 0.0), [], [qm[64:96, cp, 0:256]])
                        for hh in range(2):
                            h = cp * 2 + hh
                            diff_head(l, h,
                                      lambda masked, cp=cp: (qm[:, cp, 0:256] if masked else qk[:, CK_C0 + cp, q0:q0 + 256]),
                                      lambda kt, cp=cp: qk[:, CK_CK0 + cp, q0 + kt * 128:q0 + (kt + 1) * 128],
                                      lambda kt, cp=cp: vP[:, bi * 2 + kt, 384 + cp * 192:384 + (cp + 1) * 192],
                                      2, 256, 32 ** -0.5, None)
                        subnorm_pair(l, 6 + cp, q0, 256)
                outproj(l, xTp, PB, 0)
                norm_mod(xTp, PB, l, 1, 0)
                ffn(l, xTp, 0, False)


            SBK = [(1, 513), (513, 1025)]
            vBf = vB[:, :, :].rearrange("p t f -> p (t f)")

            def vb_pair_view(tile):
                return vBf[:, tile * 192:tile * 192 + 256].rearrange("p (g x) -> p g x", x=128)[:, :, 0:64]

            def load_ctxT(src_dT, dst, slot):
                S.dma("sp", tfp[:, slot, 0:256], src_dT)
                S.op("dve", lambda e: e.tensor_copy(out=dst, in_=tfp[:, slot, 0:256]), [tfp[:, slot, 0:256]], [dst])

            def layer_S(l):
                if DBG == 10:
                    return
                agi = aginK[l].ap(); ago = agoutK[l].ap()
                agiVr = [aginV[l][i].ap() for i in range(2)]; agoVr = [agoutV[l][i].ap() for i in range(2)]
                agiV = [a.rearrange("r c -> (r c)").rearrange("(t f) -> t f", f=384) for a in agiVr]
                agoVd = [a.rearrange("r c -> (r c)").rearrange("(t f) -> t f", f=384) for a in agoVr]
                norm_mod(xTs, SBK, l, 0, 1)
                bb = [0]
                inproj_qk(l, 0, SBK, True, bb)
                inproj_qk(l, 1, SBK, True, bb)
                for cc in range(4):
                    S.dma("sp", agi[cc * 128:(cc + 1) * 128, :], qk[:, cc, 1:1025])
                S.collective(lambda e: e.collective_compute("AllGather", ALU.bypass, replica_groups=RG, ins=[agi], outs=[ago]),
                             [agi], [ago])
                if DBG == 12:
                    return
                wvs = []
                for pn in (6, 7):
                    src = w_inp[l, :, pn * 256:(pn + 1) * 256].rearrange("(k p) n -> p k n", p=128)
                    wvs.append(W.use(S, src, 8, 256, look=7 - pn))
                vS = vloc[:, 0:3072].rearrange("p (t f) -> p t f", f=768)
                for cbase in (64, 256, 448, 640):
                    S.op("pool", lambda e, cbase=cbase: e.memset(vS[:, :, cbase:cbase + 64], 1.0), [], [vS[:, :, cbase:cbase + 64]])
                for tt in range(8):
                    sl = tt % 4
                    zz = []
                    for pi in range(2):
                        z = ps[:, (tt % 2) * 2 + pi, 0:256]
                        zz.append(z)
                        for k in range(8):
                            lh = hT[:, k, 1 + tt * 128:1 + (tt + 1) * 128]; rh = wvs[pi][:, k, :]
                            S.op("pe", lambda e, z=z, lh=lh, rh=rh, k=k: e.matmul(z, lhsT=lh, rhs=rh, start=(k == 0), stop=(k == 7)), [lh, rh], [z])
                    vt = vS[:, sl, :]
                    for mx in range(2):
                        o = vt[:, mx * 192:mx * 192 + 256].rearrange("p (g x) -> p g x", x=128)[:, :, 0:64]
                        i = zz[0][:, mx * 128:(mx + 1) * 128].rearrange("p (g x) -> p g x", x=64)
                        if mx == 0:
                            S.op("dve", lambda e, o=o, i=i: e.tensor_copy(out=o, in_=i), [zz[0][:, mx * 128:(mx + 1) * 128]], [vt[:, mx * 192:mx * 192 + 192]])
                        else:
                            S.op("dve", lambda e, o=o, i=i: e.tensor_copy(out=o, in_=i), [zz[0][:, mx * 128:(mx + 1) * 128]], [vt[:, mx * 192:mx * 192 + 192]])
                    vc3 = vt[:, 384:768].rearrange("p (a r) -> p a r", a=2)
                    i4 = zz[1].rearrange("p (a g x) -> p a g x", a=2, g=2)
                    for g in range(2):
                        o = vc3[:, :, g * 128:g * 128 + 64]; i = i4[:, :, g, :]
                        if g == 0:
                            S.op("dve", lambda e, o=o, i=i: e.tensor_copy(out=o, in_=i), [zz[1]], [vt[:, 384:768]])
                        else:
                            S.op("dve", lambda e, o=o, i=i: e.tensor_copy(out=o, in_=i), [zz[1]], [vt[:, 384:768]])
                    if DBG != 131:
                        S.dma("sp", agiV[0][tt * 128:(tt + 1) * 128, :], vt[:, 0:384])
                        S.dma("sp", agiV[1][tt * 128:(tt + 1) * 128, :], vt[:, 384:768])
                    if DBG != 132:
                        S.op("pool", lambda e, tt=tt, vt=vt: e.tensor_copy(out=vB[:, 3 + tt, :], in_=vt[:, 192:384]), [vt[:, 192:384]], [vB[:, 3 + tt, :]])
                if DBG in (13, 131, 132):
                    return
                for i in range(2):
                    S.collective(lambda e, i=i: e.collective_compute("AllGather", ALU.bypass, replica_groups=RG, ins=[agiVr[i]], outs=[agoVr[i]]),
                                 [agiVr[i]], [agoVr[i]])
                if DBG == 1:
                    return
                for panel in range(2, 6):
                    inproj_qk(l, panel, SBK, True, bb)
                if DBG == 2:
                    return
                if DBG == 22:
                    S.op("dve", lambda e: e.tensor_copy(out=gK[:, 0, 0:128], in_=kBx[:, 1, :]), [kBx[:, 1, :]], [gK[:, 0, 0:128]])
                    return
                if DBG == 23:
                    S.op("pe", lambda e: e.matmul(ps[:, 7, 0:128], lhsT=tb[:, 0, 0:128], rhs=identB[:, :], start=True, stop=True), [tb[:, 0, 0:128], identB[:, :]], [ps[:, 7, 0:128]])
                    S.op("dve", lambda e: e.tensor_copy(out=kBx[:, 0, :], in_=ps[:, 7, 0:128]), [ps[:, 7, 0:128]], [kBx[:, 0, :]])
                    return
                if DBG == 21:
                    for _i in range(8):
                        S.op("dve", lambda e: e.tensor_copy(out=kBx[:, 0, :], in_=kBx[:, 1, :]), [kBx[:, 1, :]], [kBx[:, 0, :]])
                    return
                load_ctxT(cak[l], gK[:, 0, 0:256], 0)
                load_ctxT(cbk[l], kBx[:, 0:2, :].rearrange("p t x -> p (t x)"), 1)
                if DBG in (31, 311, 313, 314):
                    return

                def ctx_v(src_d, dst_fn):
                    for t in range(2):
                        S.dma("sp", tfp[:, t, 0:128], src_d[t * 128:(t + 1) * 128, :])
                        S.op("pool", lambda e, t=t: e.tensor_copy(out=dst_fn(t), in_=tfp[:, t, 0:128].rearrange("p (g x) -> p g x", x=64)),
                             [tfp[:, t, 0:128]], [dst_fn(t)])

                gVf = gV[:, :, :].rearrange("p t f -> p (t f)")
                ctx_v(cav[l], lambda t: gVf[:, t * 192:t * 192 + 256].rearrange("p (g x) -> p g x", x=128)[:, :, 0:64])
                ctx_v(cbv[l], lambda t: vb_pair_view(t))
                if DBG == 32:
                    return
                for rr in range(4):
                    S.dma("sp", gK[:, 0, 256 + rr * 1024:256 + (rr + 1) * 1024], ago[rr * 512:rr * 512 + 128, :])
                    S.dma("sp", gV[:, 2 + rr * 8:10 + rr * 8, :],
                          agoVd[0][rr * 1024:(rr + 1) * 1024, 0:192].rearrange("(t p) f -> p t f", p=128))
                if DBG == 3:
                    return
                candK = tb[:, 1, 0:1024].rearrange("p (r e x) -> p r e x", r=4, e=2)
                candVs = [tb[:, 2, 0:768].rearrange("p (r x) -> p r x", r=4), tb[:, 0, 0:768].rearrange("p (r x) -> p r x", r=4)]
                for rr in range(4):
                    S.dma("sp", candK[:, rr, 0, :], ago[rr * 512 + 128:rr * 512 + 256, 0:128])
                    S.dma("sp", candK[:, rr, 1, :], ago[rr * 512 + 128:rr * 512 + 256, 896:1024])
                    S.dma("sp", candVs[0][:, rr, :], agoVd[0][rr * 1024 + 896:rr * 1024 + 1024, 192:384])
                    S.dma("sp", candVs[1][:, rr, :], agoVd[0][rr * 1024:rr * 1024 + 128, 192:384])
                for side in range(2):
                    e_idx = 1 - side
                    kd = kBx[:, 2 + side, :]
                    vd = vB[:, 2 if side == 0 else 11, :]
                    for rr in range(4):
                        sc = sel[:, side * 4 + rr:side * 4 + rr + 1]
                        ck = candK[:, rr, e_idx, :]
                        cv = candVs[side][:, rr, :]
                        if rr == 0:
                            S.op("dve", lambda e, kd=kd, ck=ck, sc=sc: e.tensor_scalar(out=kd, in0=ck, scalar1=sc, scalar2=None, op0=ALU.mult), [ck, sc], [kd])
                            S.op("dve", lambda e, vd=vd, cv=cv, sc=sc: e.tensor_scalar(out=vd, in0=cv, scalar1=sc, scalar2=None, op0=ALU.mult), [cv, sc], [vd])
                        else:
                            S.op("dve", lambda e, kd=kd, ck=ck, sc=sc: e.scalar_tensor_tensor(out=kd, in0=ck, scalar=sc, in1=kd, op0=ALU.mult, op1=ALU.add), [ck, sc, kd], [kd])
                            S.op("dve", lambda e, vd=vd, cv=cv, sc=sc: e.scalar_tensor_tensor(out=vd, in0=cv, scalar=sc, in1=vd, op0=ALU.mult, op1=ALU.add), [cv, sc, vd], [vd])
                if DBG == 4:
                    return
                for qb in range(2):
                    c0 = 1 + qb * 512
                    for j in range(3):
                        dense_pair(l, CK_A0 + j, lambda kt: gK[:, 0, kt * 128:(kt + 1) * 128], lambda kt: gV[:, kt, :],
                                   34, c0, 512, 0.125, j, c0, "A")
                def load_c(cp, kc):
                    load_ctxT(cck[l][cp * 128:(cp + 1) * 128, :], gK[:, kc, 0:256], 0)
                    for rr in range(4):
                        S.dma("sp", gK[:, kc, 256 + rr * 1024:256 + (rr + 1) * 1024], ago[rr * 512 + 256 + cp * 128:rr * 512 + 384 + cp * 128, :])
                load_c(0, 1)
                if DBG == 5:
                    return
                def bkey(t):
                    if t < 0:
                        return kBx[:, 2, :]
                    if t > 7:
                        return kBx[:, 3, :]
                    return qk[:, CK_BK, 1 + t * 128:1 + (t + 1) * 128]
                tbf = tb[:, :, :].rearrange("p t f -> p (t f)")
                steps = [(qb4, j, n) for qb4 in range(2) for j in range(3) for n in range(4)]

                def b_geo(si):
                    qb4, j, n = steps[si]
                    nn = qb4 * 4 + n
                    base = 0 if si % 2 == 0 else 5
                    sps = ps[:, base:base + 3, :].rearrange("p a b -> p (a b)")
                    pt = tbf[:, (si % 2) * 1280:(si % 2 + 1) * 1280]
                    return qb4, j, n, nn, sps, pt

                def b_emit_s(si):
                    qb4, j, n, nn, sps, pt = b_geo(si)
                    c0 = 1 + nn * 128
                    keys = [kBx[:, 0, :], kBx[:, 1, :], bkey(nn - 1), bkey(nn), bkey(nn + 1)]
                    for half in range(2):
                        lo = half * 64
                        for i5 in range(5):
                            o = sps[:, half * 640 + i5 * 128:half * 640 + (i5 + 1) * 128]
                            masked = i5 in (2, 4)
                            kk = keys[i5][lo:lo + 64, :]; qq = qk[lo:lo + 64, CK_B0 + j, c0:c0 + 128]
                            S.op("pe", lambda e, o=o, kk=kk, qq=qq, masked=masked: e.matmul(o, lhsT=kk, rhs=qq, start=True, stop=(not masked)), [kk, qq], [o])
                            if masked:
                                mb = bmask[:, (2 if nn == 0 else 0), :] if i5 == 2 else bmask[:, (3 if nn == 7 else 1), :]
                                S.op("pe", lambda e, o=o, mb=mb: e.matmul(o, lhsT=identB[:, :], rhs=mb, start=False, stop=True), [identB[:, :], mb], [o])

                def b_emit_exp(si):
                    qb4, j, n, nn, sps, pt = b_geo(si)
                    S.op("act", lambda e: e.activation(out=pt, in_=sps[:, 0:1280], func=AF.Exp, scale=0.125), [sps[:, 0:1280]], [pt])

                def b_emit_pv(si):
                    qb4, j, n, nn, sps, pt = b_geo(si)
                    vts = [0, 1, 2 + nn, 3 + nn, 4 + nn]
                    for half in range(2):
                        for i5 in range(5):
                            o = ps[:, 3 + half, n * 128:(n + 1) * 128]
                            vv = vB[:, vts[i5], half * 64:half * 64 + 128]; pp = pt[:, half * 640 + i5 * 128:half * 640 + (i5 + 1) * 128]
                            S.op("pe", lambda e, o=o, vv=vv, pp=pp, i5=i5: e.matmul(o, lhsT=vv, rhs=pp, start=(i5 == 0), stop=(i5 == 4)), [vv, pp], [o])
                    if n == 3:
                        finalize_pair(l, 3 + j, 1 + qb4 * 512, 512, "B", j, ob=3)

                b_emit_s(0)
                for si in range(len(steps)):
                    if si + 1 < len(steps):
                        b_emit_s(si + 1)
                    b_emit_exp(si)
                    b_emit_pv(si)
                if S1_hook[0] and l == 0:
                    S1_hook[0] = False
                    mod_busy[0] = True
                    for pn in range(24):
                        mod_tasks.append(lambda pn=pn: mod_panel(1, pn))

                    def _fin1():
                        mod_finish(1, 0, 6)
                        mod_busy[0] = False
                    mod_tasks.append(_fin1)
                for cp in range(2):
                    kc = 1 - cp
                    if cp == 1:
                        load_c(1, 0)
                    ctx_v(ccv[l][:, cp * 128:(cp + 1) * 128], lambda t: gVf[:, t * 192:t * 192 + 256].rearrange("p (g x) -> p g x", x=128)[:, :, 0:64])
                    for rr in range(4):
                        S.dma("sp", gV[:, 2 + rr * 8:10 + rr * 8, :],
                              agoVd[1][rr * 1024:(rr + 1) * 1024, cp * 192:(cp + 1) * 192].rearrange("(t p) f -> p t f", p=128))
                    for qb in range(2):
                        c0 = 1 + qb * 512
                        src = qk[64:128, CK_C0 + cp, c0:c0 + 512]
                        S.op("pool", lambda e, src=src, cp=cp: e.tensor_copy(out=qm[64:128, cp, :], in_=src), [src], [qm[64:128, cp, :]])
                        S.op("pool", lambda e, cp=cp: e.memset(qm[64:96, cp, :], 0.0), [], [qm[64:96, cp, :]])
                        for hh in range(2):
                            diff_head(l, cp * 2 + hh,
                                      lambda masked, cp=cp, c0=c0: (qm[:, cp, :] if masked else qk[:, CK_C0 + cp, c0:c0 + 512]),
                                      lambda kt, kc=kc: gK[:, kc, kt * 128:(kt + 1) * 128],
                                      lambda kt: gV[:, kt, :], 34, 512, 32 ** -0.5, None)
                        subnorm_pair(l, 6 + cp, c0, 512)
                if DBG == 7:
                    return
                outproj(l, xTs, SBK, 1)
                norm_mod(xTs, SBK, l, 1, 1)
                if DBG == 8:
                    return
                h3 = hst[:, :].rearrange("p (c t) -> p c t", t=2)
                S.op("pool", lambda e: e.tensor_copy(out=h3[:, :, 0], in_=hT[:, :, 1]), [hT[:, :, 1]], [hst[:, :]])
                S.op("pool", lambda e: e.tensor_copy(out=h3[:, :, 1], in_=hT[:, :, 1024]), [hT[:, :, 1024]], [hst[:, :]])
                hi = hin[l].ap(); ho = hout[l].ap()
                S.dma("sp", hi, hst[:, :])
                S.collective(lambda e: e.collective_compute("AllGather", ALU.bypass, replica_groups=RG, ins=[hi], outs=[ho]), [hi], [ho])
                S.dma("sp", candH[:, :, :], ho.rearrange("(r p) c -> p r c", p=128))
                for side in range(2):
                    dst = hT[:, :, 0] if side == 0 else hT[:, :, 1025]
                    for rr in range(4):
                        sc = sel[:, side * 4 + rr:side * 4 + rr + 1]
                        cv = candH[:, rr, :].rearrange("p (c t) -> p c t", t=2)[:, :, 1 - side]
                        if rr == 0:
                            S.op("dve", lambda e, dst=dst, cv=cv, sc=sc: e.tensor_scalar(out=dst, in0=cv, scalar1=sc, scalar2=None, op0=ALU.mult),
                                 [candH[:, rr, :], sc], [dst])
                        else:
                            S.op("dve", lambda e, dst=dst, cv=cv, sc=sc: e.scalar_tensor_tensor(out=dst, in0=cv, scalar=sc, in1=dst, op0=ALU.mult, op1=ALU.add),
                                 [candH[:, rr, :], sc, dst], [dst])
                if DBG == 9:
                    return
                ffn(l, xTs, 1, True)

            for pn in range(8):
                mod_panel(0, pn)
            mod_finish(0, 0, 2)
            mod_busy[0] = True
            for pn in range(8, 24):
                mod_tasks.append(lambda pn=pn: mod_panel(0, pn))

            def _fin0():
                mod_finish(0, 2, 6)
                mod_busy[0] = False
            mod_tasks.append(_fin0)
            load_x(xp, xTp, 4, 0)
            if run_s:
                load_x(xs, xTs, 8, 1)
            layer_P(0)
            mod_flush()
            if run_s:
                S1_hook[0] = True
                layer_S(0)
                mod_flush()
            else:
                for pn in range(24):
                    mod_panel(1, pn)
                mod_finish(1, 0, 6)
            layer_P(1)
            norm_mod(xTp, PB, 0, 0, 0, gain_only=pvec[:, PV_GF:PV_GF + 8])
            store_y(xTp, y_p, 4, 0)
            if run_s:
                layer_S(1)
                norm_mod(xTs, SBK, 0, 0, 1, gain_only=pvec[:, PV_GF:PV_GF + 8])
                store_y(xTs, y_s, 8, 1)

        program(DummySched())
        W.play()
        S = Sched(nc)
        program(S)
        S.emit(es)
    return nc


_NC_CACHE = {}


def _fm(v):
    return np.ascontiguousarray(np.asarray(v, np.float32).reshape(8, 128).T)


def _host_layout(inp):
    A_H, KVH, HD = 6, 2, 64
    w_in = np.asarray(inp["w_in"], np.float32)
    o_qa, o_ka, o_va = 0, 384, 512
    o_qb, o_kb, o_vb = 640, 1024, 1152
    o_qc, o_kc, o_vc = 1280, 1536, 1792

    def hcols(base, h):
        return list(range(base + h * 64, base + (h + 1) * 64))

    cols = []
    cols += list(range(o_ka, o_ka + 128))
    cols += list(range(o_kb, o_kb + 128))
    cols += list(range(o_kc, o_kc + 256))
    for j in range(3):
        cols += hcols(o_qa, j) + hcols(o_qa, 3 + j)
    for j in range(3):
        cols += hcols(o_qb, j) + hcols(o_qb, 3 + j)
    cols += list(range(o_qc, o_qc + 256))
    cols += list(range(o_va, o_va + 128)) + list(range(o_vb, o_vb + 128)) + list(range(o_vc, o_vc + 256))
    cols += list(range(o_ka, o_ka + 128)) + list(range(o_kb, o_kb + 128)) + list(range(o_kc, o_kc + 256))
    assert len(cols) == 2560
    w_inp = np.ascontiguousarray(w_in[:, :, cols])
    rows = []
    for base in (0, 384):
        for j in range(3):
            rows += hcols(base, j) + hcols(base, 3 + j)
    rows += list(range(768, 1024))
    w_outp = np.ascontiguousarray(np.asarray(inp["w_out"], np.float32)[:, rows, :])
    w_up = np.asarray(inp["w_up"], np.float32)
    w_upp = np.ascontiguousarray(
        np.stack([w_up[:, :, 0:DFF].reshape(DEPTH, D, NJ, 128), w_up[:, :, DFF:].reshape(DEPTH, D, NJ, 128)], axis=3)
        .reshape(DEPTH, D, NJ, 256))
    pvec = np.zeros((128, NPVEC), np.float32)
    for l in range(DEPTH):
        pvec[:, PV_GN + (l * 2) * 8:PV_GN + (l * 2) * 8 + 8] = _fm(inp["g_norm1"][l])
        pvec[:, PV_GN + (l * 2 + 1) * 8:PV_GN + (l * 2 + 1) * 8 + 8] = _fm(inp["g_norm2"][l])
        pvec[:, PV_BADA + l * 48:PV_BADA + (l + 1) * 48] = np.asarray(inp["b_ada"][l], np.float32).reshape(48, 128).T
        pvec[:, PV_GQK + l * 2] = np.tile(np.asarray(inp["g_qa"][l], np.float32), 2)
        pvec[:, PV_GQK + l * 2 + 1] = np.tile(np.asarray(inp["g_ka"][l], np.float32), 2)
        pvec[:, PV_GSUB + l] = np.tile(np.asarray(inp["g_subln"][l], np.float32), 2)
        cw = np.asarray(inp["conv_w"][l], np.float32)
        cb = np.asarray(inp["conv_b"][l], np.float32)
        for ag in range(2):
            for k in range(3):
                v = cw[k, ag * DFF:(ag + 1) * DFF].reshape(NJ, 128).T
                for j in range(NJ):
                    pvec[:, PV_CW + ((l * NJ + j) * 2 + ag) * 4 + k] = v[:, j]
            v = cb[ag * DFF:(ag + 1) * DFF].reshape(NJ, 128).T
            for j in range(NJ):
                pvec[:, PV_CW + ((l * NJ + j) * 2 + ag) * 4 + 3] = v[:, j]
        sk = np.asarray(inp["sink_b"][l], np.float32)
        for j in range(3):
            pvec[0:64, PV_SINK + l * 3 + j] = sk[j]
            pvec[64:128, PV_SINK + l * 3 + j] = sk[3 + j]
            pvec[0:64, PV_SINKX + l * 3 + j] = sk[3 + j]
            pvec[64:128, PV_SINKX + l * 3 + j] = sk[j]
    pvec[:, PV_GF:PV_GF + 8] = _fm(inp["g_final"])
    lamv = np.stack([np.stack([np.asarray(inp[n][l], np.float32) for n in ("lam_q1", "lam_k1", "lam_q2", "lam_k2")])
                     for l in range(DEPTH)]).reshape(1, -1)
    cm = np.zeros((128, 4, 128), np.float32)
    for m in range(128):
        d = m % 64
        partner = m + 16 if (d % 32) < 16 else m - 16
        cm[partner, 0, m] = 1.0
        d = m % 32
        partner = m + 8 if (d % 16) < 8 else m - 8
        cm[partner, 1, m] = 1.0
    for a in range(2):
        cm[a * 64:(a + 1) * 64, 2, a * 64:(a + 1) * 64] = 1.0
    cm[:, 3, :] = 1.0
    shared = {
        "w_ada": np.ascontiguousarray(np.asarray(inp["w_ada"], np.float32)),
        "w_inp": w_inp, "w_outp": w_outp, "w_upp": w_upp,
        "w_downp": np.ascontiguousarray(np.asarray(inp["w_down"], np.float32)),
        "pvec": pvec, "lamv": np.ascontiguousarray(lamv),
        "gka_row": np.ascontiguousarray(np.asarray(inp["g_ka"], np.float32)),
        "cmat": cm.astype(ml_dtypes.bfloat16),
    }
    jj = np.arange(128)[:, None]
    ii = np.arange(128)[None, :]
    triL = (jj >= ii).astype(np.float32)
    triR = (jj <= ii).astype(np.float32)
    per_core = []
    for c in range(8):
        b, r = c // 4, c % 4
        m = dict(shared)
        m["xp"] = np.ascontiguousarray(np.asarray(inp["x_prompt"], np.float32)[2 * c:2 * c + 2].reshape(512, D))
        m["xs"] = np.ascontiguousarray(np.asarray(inp["x_sample"], np.float32)[b, r * 1024:(r + 1) * 1024])
        m["cakT"] = np.ascontiguousarray(np.asarray(inp["cache_a_k"], np.float32)[b].reshape(DEPTH, 256, 128).transpose(0, 2, 1))
        m["cav"] = np.ascontiguousarray(np.asarray(inp["cache_a_v"], np.float32)[b].reshape(DEPTH, 256, 128))
        m["cbkT"] = np.ascontiguousarray(np.asarray(inp["cache_b_k"], np.float32)[b].reshape(DEPTH, 256, 128).transpose(0, 2, 1))
        m["cbv"] = np.ascontiguousarray(np.asarray(inp["cache_b_v"], np.float32)[b].reshape(DEPTH, 256, 128))
        m["cckT"] = np.ascontiguousarray(np.asarray(inp["cache_c_k"], np.float32)[b].reshape(DEPTH, 256, 256).transpose(0, 2, 1))
        m["ccv"] = np.ascontiguousarray(np.asarray(inp["cache_c_v"], np.float32)[b].reshape(DEPTH, 256, 256))
        cond2 = np.stack([np.asarray(inp["c_ctx"], np.float32), np.asarray(inp["c"], np.float32)[b]], axis=1)
        m["cond2T"] = np.ascontiguousarray(cond2.reshape(8, 128, 2).transpose(1, 0, 2))
        t = np.arange(r * 1024, (r + 1) * 1024)
        rows_, cols_ = (t // 64).astype(np.float64), (t % 64).astype(np.float64)
        rp = np.zeros((128, 4, 1024), np.float32)
        for p in range(128):
            d = p % 64
            pos = rows_ if d < 32 else cols_
            i = d % 16
            inv = np.float32(THETA) ** (-np.float32(i) / np.float32(16))
            ang = pos.astype(np.float32) * np.float32(inv)
            sgn = -1.0 if (d % 32) < 16 else 1.0
            rp[p, 0] = np.cos(ang); rp[p, 1] = sgn * np.sin(ang)
            d = p % 32
            pos = rows_ if d < 16 else cols_
            i = d % 8
            inv = np.float32(THETA) ** (-np.float32(i) / np.float32(8))
            ang = pos.astype(np.float32) * np.float32(inv)
            sgn = -1.0 if (d % 16) < 8 else 1.0
            rp[p, 2] = np.cos(ang); rp[p, 3] = sgn * np.sin(ang)
        m["rope"] = rp.astype(ml_dtypes.bfloat16)
        bm = np.stack([triL, triR, triL * (1.0 if r > 0 else 0.0), triR * (1.0 if r < 3 else 0.0)], axis=1)
        m["bmask"] = ((bm - 1.0) * 30000.0).astype(ml_dtypes.bfloat16)
        sl = np.zeros((128, 8), np.float32)
        if r > 0:
            sl[:, r - 1] = 1.0
        if r < 3:
            sl[:, 4 + r + 1] = 1.0
        m["sel"] = sl
        per_core.append(m)
    return per_core


RUN_S = True
DBG = 0


def kernel(**inputs):
    key = ("nc", RUN_S, DBG)
    if key not in _NC_CACHE:
        _NC_CACHE[key] = build_program(RUN_S)
    nc = _NC_CACHE[key]
    in_maps = _host_layout(inputs)
    res = run_bass_kernel_spmd(nc, in_maps, core_ids=list(range(8)))
    r = res.results
    y_prompt = np.concatenate([np.asarray(r[c]["y_p"], np.float32).reshape(2, 256, D) for c in range(8)], axis=0)
    if RUN_S:
        y_sample = np.stack([np.concatenate([np.asarray(r[b * 4 + q]["y_s"], np.float32) for q in range(4)], axis=0)
                             for b in range(2)], axis=0)
    else:
        y_sample = np.zeros((2, 4096, D), np.float32)

    def gat(name, kvh, hd):
        return np.concatenate([np.asarray(r[c][name], np.float32).reshape(2, DEPTH, 256, kvh, hd) for c in range(8)], axis=0)

    return (y_prompt, y_sample, gat("o_ak", 2, 64), gat("o_av", 2, 64), gat("o_bk", 2, 64), gat("o_bv", 2, 64),
            gat("o_ck", 4, 64), gat("o_cv", 4, 64))
```
